# Optimizing a Trainium2 kernel written in Bass

```python
import functools
import jax, jax.numpy as jnp
from jax import lax
import numpy as np

D_MODEL = 1024
BATCH = 16
SEQ = 4096
DEPTH = 1

GRID_W = 64
CTX_LEN = 256
POOL_WIDTH = 512
POOL_WINDOWS = (2, 4, 8, 16)
POOL_GROUPS = len(POOL_WINDOWS)
POOL_GC = POOL_WIDTH // POOL_GROUPS
N_HEADS = 16
N_KV_HEADS = 4
GROUP = N_HEADS // N_KV_HEADS
HEAD_DIM = 64
ATTN_WIDTH = N_HEADS * HEAD_DIM
KV_WIDTH = N_KV_HEADS * HEAD_DIM
WINDOW = 128
Q_BLOCK = 128
ROPE_BASE = 10000.0
ROPE_AXIS_DIM = HEAD_DIM // 2
EPS = 1e-6
NEG_INF = -1e30
SPLITS = (POOL_WIDTH, POOL_WIDTH, ATTN_WIDTH, KV_WIDTH, KV_WIDTH, ATTN_WIDTH, 2 * D_MODEL)
IN_WIDTH = POOL_WIDTH * 2 + ATTN_WIDTH * 2 + KV_WIDTH * 2 + 2 * D_MODEL
K_OFF = 2 * POOL_WIDTH + ATTN_WIDTH
V_OFF = K_OFF + KV_WIDTH

kernel_name = "hybrid_pool_window_gqa_dit_block"


def rms_norm(x, g):
    xf = x.astype(jnp.float32)
    y = xf * lax.rsqrt(jnp.mean(xf * xf, axis=-1, keepdims=True) + EPS)
    return (y * g.astype(jnp.float32)).astype(x.dtype)


def adaln(cond, w_mod, b_mod):
    m = jax.nn.silu(cond) @ w_mod + b_mod
    shift, scale, gate = jnp.split(m, 3, axis=-1)
    return shift, scale, gate


def modulate(xn, shift, scale):
    return xn * (1.0 + scale) + shift


def split_columns(p):
    outs, off = [], 0
    for w in SPLITS:
        outs.append(p[..., off:off + w])
        off += w
    return outs


def rope_tables(pos):
    n_freq = ROPE_AXIS_DIM // 2
    freqs = ROPE_BASE ** (-jnp.arange(n_freq, dtype=jnp.float32) / n_freq)
    ang = pos.astype(jnp.float32)[:, None] * freqs[None, :]
    return jnp.cos(ang), jnp.sin(ang)


def _rotate(xh, cos, sin):
    half = xh.shape[-1] // 2
    x1, x2 = xh[..., :half], xh[..., half:]
    return jnp.concatenate([x1 * cos - x2 * sin, x2 * cos + x1 * sin], axis=-1)


def apply_axial_rope(x, cos_r, sin_r, cos_c, sin_c):
    shp = (1, x.shape[1]) + (1,) * (x.ndim - 3) + (ROPE_AXIS_DIM // 2,)
    r = lambda t: t.reshape(shp).astype(x.dtype)
    return jnp.concatenate([_rotate(x[..., :ROPE_AXIS_DIM], r(cos_r), r(sin_r)),
                            _rotate(x[..., ROPE_AXIS_DIM:], r(cos_c), r(sin_c))], axis=-1)


def pool_mixer(xa, pool_w, pool_scale):
    B, L, _ = xa.shape
    xg = xa.reshape(B, L, POOL_GROUPS, POOL_GC)
    csum = jnp.cumsum(xg.astype(jnp.float32), axis=1)
    P = jnp.concatenate([jnp.zeros_like(csum[:, :1]), csum], axis=1)
    t = jnp.arange(L)
    outs = []
    for g, w in enumerate(POOL_WINDOWS):
        lo = jnp.clip(t - w // 2, 0, L)
        hi = jnp.clip(t + w // 2, 0, L)
        cnt = (hi - lo).astype(jnp.float32)[None, :, None]
        mean = (P[:, hi, g] - P[:, lo, g]) / cnt
        outs.append(mean - xg[:, :, g].astype(jnp.float32))
    pooled = jnp.stack(outs, axis=2).astype(xa.dtype)
    y = jnp.einsum('bsgc,gcd->bsgd', pooled, pool_w).reshape(B, L, POOL_WIDTH)
    return y * pool_scale


def windowed_attention(q, k, v, k_ctx, v_ctx, sink):
    B, L = q.shape[:2]
    C = k_ctx.shape[1]
    nblk = L // Q_BLOCK
    scale = HEAD_DIM ** -0.5
    pad = ((0, 0), (Q_BLOCK, Q_BLOCK), (0, 0), (0, 0))
    kp = jnp.pad(k, pad)
    vp = jnp.pad(v, pad)
    sink_col = jnp.broadcast_to(sink.astype(jnp.float32).reshape(1, N_KV_HEADS, GROUP, 1, 1),
                                (B, N_KV_HEADS, GROUP, Q_BLOCK, 1))

    def block(i):
        start = i * Q_BLOCK
        qb = lax.dynamic_slice_in_dim(q, start, Q_BLOCK, axis=1)
        kb = lax.dynamic_slice_in_dim(kp, start, 3 * Q_BLOCK, axis=1)
        vb = lax.dynamic_slice_in_dim(vp, start, 3 * Q_BLOCK, axis=1)
        qpos = start + jnp.arange(Q_BLOCK)
        kpos = start - Q_BLOCK + jnp.arange(3 * Q_BLOCK)
        valid = ((jnp.abs(qpos[:, None] - kpos[None, :]) <= WINDOW)
                 & (kpos >= 0)[None, :] & (kpos < L)[None, :])
        s_lat = jnp.einsum('bqhgd,bkhd->bhgqk', qb, kb).astype(jnp.float32) * scale
        s_lat = jnp.where(valid, s_lat, NEG_INF)
        s_ctx = jnp.einsum('bqhgd,bkhd->bhgqk', qb, k_ctx).astype(jnp.float32) * scale
        p = jax.nn.softmax(jnp.concatenate([s_lat, s_ctx, sink_col], axis=-1), axis=-1).astype(v.dtype)
        p_lat = p[..., :3 * Q_BLOCK]
        p_ctx = p[..., 3 * Q_BLOCK:3 * Q_BLOCK + C]
        return (jnp.einsum('bhgqk,bkhd->bqhgd', p_lat, vb)
                + jnp.einsum('bhgqk,bkhd->bqhgd', p_ctx, v_ctx))

    o = lax.map(block, jnp.arange(nblk))
    return jnp.moveaxis(o, 0, 1).reshape(B, L, N_KV_HEADS, GROUP, HEAD_DIM)


def context_attention(q, k, v, sink):
    B, C = q.shape[:2]
    s = jnp.einsum('bqhgd,bkhd->bhgqk', q, k).astype(jnp.float32) * (HEAD_DIM ** -0.5)
    sink_col = jnp.broadcast_to(sink.astype(jnp.float32).reshape(1, N_KV_HEADS, GROUP, 1, 1),
                                (B, N_KV_HEADS, GROUP, C, 1))
    p = jax.nn.softmax(jnp.concatenate([s, sink_col], axis=-1), axis=-1).astype(v.dtype)
    return jnp.einsum('bhgqk,bkhd->bqhgd', p[..., :C], v)


def attend_latent(q, k, v, k_ctx, v_ctx, sink, cos_r, sin_r, cos_c, sin_c):
    B, L, _ = q.shape
    q = apply_axial_rope(q.reshape(B, L, N_KV_HEADS, GROUP, HEAD_DIM), cos_r, sin_r, cos_c, sin_c)
    k = apply_axial_rope(k.reshape(B, L, N_KV_HEADS, HEAD_DIM), cos_r, sin_r, cos_c, sin_c)
    v = v.reshape(B, L, N_KV_HEADS, HEAD_DIM)
    o = windowed_attention(q, k, v, k_ctx, v_ctx, sink)
    return o.reshape(B, L, ATTN_WIDTH)


def attend_context(q, k, v, sink):
    B, C, _ = q.shape
    o = context_attention(q.reshape(B, C, N_KV_HEADS, GROUP, HEAD_DIM),
                          k.reshape(B, C, N_KV_HEADS, HEAD_DIM),
                          v.reshape(B, C, N_KV_HEADS, HEAD_DIM), sink)
    return o.reshape(B, C, ATTN_WIDTH)


def mixer_sublayer(h, w_in, pool_w, pool_scale, w_branch_a, w_branch_b, w_out, attend):
    xa, ga, q, k, v, gb, gm = split_columns(h @ w_in)
    y_a = pool_mixer(xa, pool_w, pool_scale) * jax.nn.silu(ga)
    y_b = attend(q, k, v) * jax.nn.silu(gb)
    g = jax.nn.sigmoid(gm)
    merged = g[..., :D_MODEL] * (y_a @ w_branch_a) + g[..., D_MODEL:] * (y_b @ w_branch_b)
    return merged @ w_out


def setup_inputs(seed: int = 0) -> dict:
    key = jax.random.key(seed)
    ks = jax.random.split(key, 16)
    f32 = jnp.float32
    nrm = lambda k, shp, s: jax.random.normal(k, shp, f32) * s
    return {
        "x": nrm(ks[0], (BATCH, SEQ, D_MODEL), 1.0),
        "c": nrm(ks[1], (BATCH, D_MODEL), 1.0),
        "ctx": nrm(ks[2], (BATCH, CTX_LEN, D_MODEL), 1.0),
        "c_ctx": nrm(ks[3], (D_MODEL,), 1.0),
        "w_mod": nrm(ks[4], (DEPTH, D_MODEL, 3 * D_MODEL), 0.5 * D_MODEL ** -0.5),
        "b_mod": nrm(ks[5], (DEPTH, 3 * D_MODEL), 0.02),
        "norm_pre_g": 1.0 + nrm(ks[6], (DEPTH, D_MODEL), 0.05),
        "norm_post_g": 1.0 + nrm(ks[7], (DEPTH, D_MODEL), 0.05),
        "w_in": nrm(ks[8], (DEPTH, D_MODEL, IN_WIDTH), D_MODEL ** -0.5),
        "pool_w": nrm(ks[9], (DEPTH, POOL_GROUPS, POOL_GC, POOL_GC), POOL_GC ** -0.5),
        "pool_scale": 0.5 + nrm(ks[10], (DEPTH, POOL_WIDTH), 0.1),
        "sink": nrm(ks[11], (DEPTH, N_HEADS), 0.5),
        "w_branch_a": nrm(ks[12], (DEPTH, POOL_WIDTH, D_MODEL), POOL_WIDTH ** -0.5),
        "w_branch_b": nrm(ks[13], (DEPTH, ATTN_WIDTH, D_MODEL), ATTN_WIDTH ** -0.5),
        "w_out": nrm(ks[14], (DEPTH, D_MODEL, D_MODEL), D_MODEL ** -0.5),
    }


def reference(x, c, ctx, c_ctx, w_mod, b_mod, norm_pre_g, norm_post_g, w_in, pool_w, pool_scale,
              sink, w_branch_a, w_branch_b, w_out):
    B, L, _ = x.shape
    C = ctx.shape[1]
    ROWS = L // GRID_W
    row = jnp.repeat(jnp.arange(ROWS), GRID_W)
    col = jnp.tile(jnp.arange(GRID_W), ROWS)
    cos_r, sin_r = rope_tables(row)
    cos_c, sin_c = rope_tables(col)

    for i in range(DEPTH):
        shift, scale, gate = adaln(c, w_mod[i], b_mod[i])
        c_shift, c_scale, c_gate = adaln(c_ctx, w_mod[i], b_mod[i])

        h_ctx = modulate(rms_norm(ctx, norm_pre_g[i]), c_shift, c_scale)
        k_ctx = (h_ctx @ w_in[i][:, K_OFF:K_OFF + KV_WIDTH]).reshape(B, C, N_KV_HEADS, HEAD_DIM)
        v_ctx = (h_ctx @ w_in[i][:, V_OFF:V_OFF + KV_WIDTH]).reshape(B, C, N_KV_HEADS, HEAD_DIM)

        h = modulate(rms_norm(x, norm_pre_g[i]), shift[:, None, :], scale[:, None, :])
        attend = functools.partial(attend_latent, k_ctx=k_ctx, v_ctx=v_ctx, sink=sink[i],
                                   cos_r=cos_r, sin_r=sin_r, cos_c=cos_c, sin_c=sin_c)
        out = mixer_sublayer(h, w_in[i], pool_w[i], pool_scale[i], w_branch_a[i], w_branch_b[i],
                             w_out[i], attend)
        x_new = x + gate[:, None, :] * rms_norm(out, norm_post_g[i])

        if i < DEPTH - 1:
            attend_c = functools.partial(attend_context, sink=sink[i])
            out_c = mixer_sublayer(h_ctx, w_in[i], pool_w[i], pool_scale[i], w_branch_a[i],
                                   w_branch_b[i], w_out[i], attend_c)
            ctx = ctx + c_gate * rms_norm(out_c, norm_post_g[i])
        x = x_new
    return x
```

```python
import numpy as np
from contextlib import ExitStack

import concourse.bass as bass
import concourse.mybir as mybir
from concourse.ap import AP
from concourse.bass_utils import run_bass_kernel_spmd

F32 = mybir.dt.float32
BF16 = mybir.dt.bfloat16
AF = mybir.ActivationFunctionType
ALU = mybir.AluOpType

D = 1024
NCORES = 8
BPC = 2
CTX = 256
EPS = 1e-6
NEG = -30000.0
HP = [0, 4, 1, 5, 2, 6, 3, 7, 8, 12, 9, 13, 10, 14, 11, 15]
POOL_WINDOWS = (2, 4, 8, 16)
ENGINES = ("pe", "act", "dve", "pool", "sp")


class Buf:
    __slots__ = ("name", "last_w", "readers", "excl", "touch")

    def __init__(self, name, excl=False):
        self.name = name
        self.last_w = None
        self.readers = []
        self.touch = 0
        self.excl = excl


class Tracker:
    def __init__(self):
        self.ops = {e: [] for e in ENGINES}
        self.count = {}
        self.known = {e: {} for e in ENGINES}
        self.snap = {}
        self.opn = 0

    def _need(self, eng, ev, waits):
        if ev is None:
            return
        key, val = ev
        if eng == "pe" and key == "pe":
            return
        if self.known[eng].get(key, 0) >= val:
            return
        if val > self.count.get(key, 0):
            raise RuntimeError(f"pending event {ev} needed by {eng}")
        waits[key] = max(waits.get(key, 0), val)

    def _apply_waits(self, eng, waits):
        kn = self.known[eng]
        for key, val in waits.items():
            if kn.get(key, 0) < val:
                kn[key] = val
            sn = self.snap.get((key, val))
            if sn:
                for k2, v2 in sn.items():
                    if kn.get(k2, 0) < v2:
                        kn[k2] = v2

    def emit(self, eng, fn, reads=(), writes=(), inc=True, dma_sem=None):
        waits = {}
        self.opn += 1
        for b in reads:
            b.touch = self.opn
        for b in writes:
            b.touch = self.opn
        for b in reads:
            self._need(eng, b.last_w, waits)
            if b.excl:
                for r in b.readers:
                    if r[0] != eng:
                        self._need(eng, r, waits)
        for b in writes:
            self._need(eng, b.last_w, waits)
            for r in b.readers:
                self._need(eng, r, waits)
        self._apply_waits(eng, waits)
        if dma_sem is not None:
            self.count[dma_sem] = self.count.get(dma_sem, 0) + 16
            ev = (dma_sem, self.count[dma_sem])
            incspec = (dma_sem, 16)
        elif inc:
            self.count[eng] = self.count.get(eng, 0) + 1
            ev = (eng, self.count[eng])
            incspec = (eng, 1)
        else:
            ev = (eng, self.count.get(eng, 0) + 1)
            incspec = None
        self.snap[ev] = dict(self.known[eng])
        for b in reads:
            b.readers.append(ev)
        for b in writes:
            b.last_w = ev
            b.readers = []
        self.ops[eng].append((sorted(waits.items()), fn, incspec))
        return ev

    def barrier(self):
        allv = dict(self.count)
        for eng in ENGINES:
            waits = {}
            for key, val in allv.items():
                if val > 0 and self.known[eng].get(key, 0) < val:
                    if eng == "pe" and key == "pe":
                        continue
                    waits[key] = val
            self._apply_waits(eng, waits)
            if waits:
                self.ops[eng].append((sorted(waits.items()), None, None))

    def final_wait(self, eng, keys):
        waits = {k: self.count[k] for k in keys if self.count.get(k, 0) > 0}
        self._apply_waits(eng, waits)
        self.ops[eng].append((sorted(waits.items()), None, None))

    def replay_engine(self, eng, eobj, sems):
        for waits, fn, incspec in self.ops[eng]:
            for key, val in waits:
                eobj.wait_ge(sems[key], val)
            if fn is None:
                continue
            ins = fn(eobj)
            if incspec is not None:
                ins.then_inc(sems[incspec[0]], incspec[1])


def rope_tables(L):
    nt = L // 128
    t = np.arange(L)
    row = (t // 64).astype(np.float64)
    col = (t % 64).astype(np.float64)
    freqs = 10000.0 ** (-np.arange(16, dtype=np.float64) / 16.0)
    out = np.zeros((L, 2, 2, 2, 16), np.float64)
    for a, pos in enumerate((row, col)):
        ang = (pos.astype(np.float32)[:, None] * freqs.astype(np.float32)[None, :]).astype(np.float32).astype(np.float64)
        out[:, 0, a, 0, :] = np.cos(ang)
        out[:, 0, a, 1, :] = np.cos(ang)
        out[:, 1, a, 0, :] = -np.sin(ang)
        out[:, 1, a, 1, :] = np.sin(ang)
    return out.reshape(nt, 128, 128).astype(np.float32)


def band_tables():
    M = np.zeros((128, 20, 128), np.float64)
    ss = np.arange(128)[:, None]
    tt = np.arange(128)[None, :]
    eye = (ss == tt).astype(np.float64)
    for g, w in enumerate(POOL_WINDOWS):
        h = w // 2
        M[:, 5 * g + 0, :] = ((ss >= tt - h) & (ss < tt + h)) / w - eye
        lo = np.maximum(tt - h, 0)
        M[:, 5 * g + 1, :] = ((ss >= lo) & (ss < tt + h)) / (tt + h - lo) - eye
        hi = np.minimum(tt + h, 128)
        M[:, 5 * g + 2, :] = ((ss >= tt - h) & (ss < hi)) / (hi - (tt - h)) - eye
        M[:, 5 * g + 3, :] = ((ss - 128 >= tt - h) & (ss - 128 < tt + h)) / w
        M[:, 5 * g + 4, :] = ((ss + 128 >= tt - h) & (ss + 128 < tt + h)) / w
    return M.reshape(128, 20 * 128).astype(np.float32)


def mask_tables():
    kk = np.arange(128)[:, None]
    qq = np.arange(128)[None, :]
    m = np.zeros((128, 2, 128), np.float32)
    m[:, 0, :] = np.where(qq <= kk, 0.0, NEG)
    m[:, 1, :] = np.where(kk <= qq, 0.0, NEG)
    return m.reshape(128, 256)


def build_program(L, STOP=0):
    NT = L // 128
    nc = bass.Bass("TRN2", target_bir_lowering=False)

    def din(name, shape):
        return nc.dram_tensor(name, list(shape), F32, kind="ExternalInput").ap()

    x_d = din("x", [BPC, L, D])
    ctx_d = din("ctx", [BPC, CTX, D])
    cT_d = din("cT", [128, 32])
    wtok_d = din("w_tok", [D, 3072])
    wfeat_d = din("w_feat", [D, 2560])
    wa_d = din("w_a", [512, D])
    wb_d = din("w_b", [D, D])
    wout_d = din("w_o", [D, D])
    poolw_d = din("pool_w", [128, 512])
    wmod_d = din("w_mod", [D, 3072])
    bmodT_d = din("b_modT", [128, 16])
    bgate_d = din("b_gate", [1, D])
    gpreT_d = din("g_preT", [128, 8])
    gpost_d = din("g_post", [1, D])
    pscale_d = din("pool_scale", [1, 512])
    sink_d = din("sink", [1, 16])
    rope_d = din("rope", [NT, 128, 128])
    band_d = din("band", [128, 2560])
    mask_d = din("mask", [128, 256])
    y_d = nc.dram_tensor("y", [BPC, L, D], F32, kind="ExternalOutput").ap()
    ggs_d = nc.dram_tensor("gg_scr", [BPC, D], F32, kind="ExternalOutput").ap()

    tr = Tracker()
    E = tr.emit
    es = ExitStack()
    semkeys = list(ENGINES)

    def pstep(t):
        return t[:].ap[0][0]

    def view(t, off, dims):
        a = t[:]
        return AP(a.tensor, a.offset + off, [[a.ap[0][0], 128]] + [list(d) for d in dims])

    def pview(t, p0, pn, off, dims):
        a = t[p0:p0 + pn]
        return AP(a.tensor, a.offset + off, [[a.ap[0][0], pn]] + [list(d) for d in dims])

    with es:
        def sb(name, shape, dt, stack=es):
            return stack.enter_context(nc.sbuf_tensor("s_" + name, list(shape), dt))

        Wtok = sb("Wtok", [128, 8, 3072], BF16)
        Wfeat = sb("Wfeat", [128, 8, 2560], BF16)
        Wa = sb("Wa", [128, 4, D], BF16)
        Wb = sb("Wb", [128, 8, D], BF16)
        Wo = sb("Wo", [128, 8, D], BF16)
        Wp = sb("Wp", [128, 4, 128], BF16)
        ident = sb("ident", [128, 128], BF16)
        band = sb("band", [128, 20, 128], BF16)
        maskb = sb("maskb", [128, 2, 128], BF16)
        mh = sb("mh", [128, 1], F32)
        gs = sb("gs", [128, 32], F32)
        sh = sb("sh", [128, 32], F32)
        gg = sb("gg", [128, D], F32)
        esink = sb("esink", [128, 16], F32)
        kTc1 = sb("kTc", [128, 2, CTX], BF16)
        vc1 = sb("vc", [128, 2, 4, 66], BF16)
        kTc = [kTc1, kTc1]
        vc = [vc1, vc1]

        NPB, NTB = 7, 1
        pbank = [es.enter_context(nc.psum_tensor(f"pb{i}", [128, 512], F32)) for i in range(NPB)]
        tbank = [es.enter_context(nc.psum_tensor(f"tb{i}", [128, 1024], BF16)) for i in range(NTB)]
        pbuf = [Buf(f"pb{i}", excl=True) for i in range(NPB)]
        tbuf = [Buf(f"tb{i}", excl=True) for i in range(NTB)]

        def _lru(bufs):
            i = min(range(len(bufs)), key=lambda j: bufs[j].touch)
            tr.opn += 1
            bufs[i].touch = tr.opn
            return i

        def next_bank():
            i = _lru(pbuf)
            return pbank[i], pbuf[i]

        def next_tbank():
            i = _lru(tbuf)
            return tbank[i], tbuf[i]

        B = {}

        def bf(name):
            if name not in B:
                B[name] = Buf(name)
            return B[name]

        dma_sems = []

        def dsem(name):
            if name not in dma_sems:
                dma_sems.append(name)
            return name

        E("pool", lambda e: e.memset(mh[:], -0.5), writes=[bf("mh")])

        def ld(dst_ap, src_ap, bname):
            s_ = dsem("d_" + bname)
            E("sp", lambda e: e.dma_start(out=dst_ap, in_=src_ap), writes=[bf(bname)], dma_sem=s_)

        with ExitStack() as ps_:
            identf = sb("identf", [128, 128], F32, ps_)
            NS = 5
            stg = [sb(f"stg{i}", [128, 1536], F32, ps_) for i in range(NS)]
            pscl = sb("pscl", [128, 512], F32, ps_)
            sinkb = sb("sinkb", [128, 16], F32, ps_)

            E("pool", lambda e: e.memset(identf[:], 0.0), writes=[bf("identf")])
            E("pool", lambda e: e.affine_select(out=identf[:], in_=identf[:], pattern=[[-1, 128]],
                                                compare_op=ALU.not_equal, fill=1.0, base=0, channel_multiplier=1),
              reads=[bf("identf")], writes=[bf("identf")])
            E("pool", lambda e: e.tensor_copy(out=ident[:], in_=identf[:]), reads=[bf("identf")], writes=[bf("ident")])
            ld(pscl[:], pscale_d.partition_broadcast(128), "pscl")
            ld(sinkb[:], sink_d.partition_broadcast(128), "sinkb")

            cast_rr = {"i": 0, "s": 0}

            def load_cast(dst_ap, src_ap, n, dst_buf, mul_ap=None):
                si = cast_rr["s"] % NS
                cast_rr["s"] += 1
                sbuf_ = bf(f"stg{si}")
                s_ = dsem(f"d_stg{si}")
                E("sp", lambda e: e.dma_start(out=stg[si][:, 0:n], in_=src_ap), writes=[sbuf_], dma_sem=s_)
                if mul_ap is not None:
                    E("dve", lambda e: e.tensor_tensor(out=dst_ap, in0=stg[si][:, 0:n], in1=mul_ap, op=ALU.mult),
                      reads=[sbuf_, bf("pscl")], writes=[dst_buf])
                    return
                ci = cast_rr["i"] % 2
                cast_rr["i"] += 1
                if ci == 0:
                    E("dve", lambda e: e.tensor_copy(out=dst_ap, in_=stg[si][:, 0:n]), reads=[sbuf_], writes=[dst_buf])
                else:
                    E("act", lambda e: e.activation(out=dst_ap, in_=stg[si][:, 0:n], func=AF.Copy), reads=[sbuf_], writes=[dst_buf])

            for k in range(8):
                for h in range(2):
                    load_cast(Wtok[:, k, h * 1536:(h + 1) * 1536], wtok_d[k * 128:(k + 1) * 128, h * 1536:(h + 1) * 1536], 1536, bf("Wtok"))
            for k in range(8):
                for h in range(2):
                    load_cast(Wfeat[:, k, h * 1280:(h + 1) * 1280], wfeat_d[k * 128:(k + 1) * 128, h * 1280:(h + 1) * 1280], 1280, bf("Wfeat"))
            for k in range(4):
                load_cast(Wa[:, k, :], wa_d[k * 128:(k + 1) * 128, :], 1024, bf("Wa"))
            for k in range(8):
                load_cast(Wb[:, k, :], wb_d[k * 128:(k + 1) * 128, :], 1024, bf("Wb"))
            for k in range(8):
                load_cast(Wo[:, k, :], wout_d[k * 128:(k + 1) * 128, :], 1024, bf("Wo"))
            load_cast(view(Wp, 0, [[1, 512]]), poolw_d, 512, bf("Wp"), mul_ap=pscl[:])
            load_cast(view(band, 0, [[1, 1536]]), band_d[:, 0:1536], 1536, bf("band"))
            load_cast(view(band, 1536, [[1, 1024]]), band_d[:, 1536:2560], 1024, bf("band"))
            load_cast(view(maskb, 0, [[1, 256]]), mask_d, 256, bf("maskb"))
            E("act", lambda e: e.activation(out=esink[:], in_=sinkb[:], func=AF.Exp), reads=[bf("sinkb")], writes=[bf("esink")])
            tr.barrier()

        with ExitStack() as ps_:
            wst = [sb(f"wst{i}", [128, 8, 512], F32, ps_) for i in range(2)]
            cT = sb("cT", [128, 32], F32, ps_)
            th = sb("th", [128, 32], F32, ps_)
            sc = sb("sc", [128, 32], F32, ps_)
            screp = [sb(f"screp{j}", [128, 8, 128], F32, ps_) for j in range(BPC)]
            mT = sb("mT", [128, 64], F32, ps_)
            bmodT = sb("bmodT", [128, 16], F32, ps_)
            gpreT = sb("gpreT", [128, 8], F32, ps_)
            bgate = sb("bgate", [128, D], F32, ps_)
            gpost = sb("gpost", [128, D], F32, ps_)
            gtmp = sb("gtmp", [128, 512], F32, ps_)
            ggt1 = sb("ggt", [128, D], F32, ps_)
            ggt = [ggt1, ggt1]

            ld(cT[:], cT_d, "cT")
            def load_wmod(blk):
                wi = blk % 2
                E("sp", lambda e: e.dma_start(
                    out=wst[wi][:], in_=wmod_d[:, blk * 512:(blk + 1) * 512].rearrange("(k p) n -> p k n", p=128)),
                  writes=[bf(f"wst{wi}")], dma_sem=dsem(f"d_wst{wi}"))
            load_wmod(0)
            load_wmod(1)
            ld(bmodT[:], bmodT_d, "bmodT")
            ld(gpreT[:], gpreT_d, "gpreT")
            ld(bgate[:], bgate_d.partition_broadcast(128), "bgate")
            ld(gpost[:], gpost_d.partition_broadcast(128), "gpost")

            E("act", lambda e: e.activation(out=th[:], in_=cT[:], func=AF.Tanh, scale=0.5), reads=[bf("cT")], writes=[bf("th")])
            E("dve", lambda e: e.scalar_tensor_tensor(out=sc[:], in0=th[:], scalar=1.0, in1=cT[:], op0=ALU.add, op1=ALU.mult),
              reads=[bf("th"), bf("cT")], writes=[bf("sc")])
            E("dve", lambda e: e.tensor_scalar_mul(out=sc[:], in0=sc[:], scalar1=0.5), reads=[bf("sc")], writes=[bf("sc")])
            for j in range(BPC):
                E("dve", lambda e, j=j: e.tensor_copy(out=screp[j][:], in_=view(sc, j, [[4, 8], [0, 128]])),
                  reads=[bf("sc")], writes=[bf(f"screp{j}")])

            psA, psAb = next_bank()
            psG = [[next_bank() for nb in range(2)] for j in range(BPC)]
            for blk in range(6):
                wi = blk % 2
                if blk < 4:
                    for cc in range(4):
                        ch = blk * 4 + cc
                        for k in range(8):
                            E("pe", lambda e, wi=wi, cc=cc, ch=ch, k=k: e.matmul(
                                out=psA[:, ch * 4:(ch + 1) * 4], lhsT=wst[wi][:, k, cc * 128:(cc + 1) * 128],
                                rhs=sc[:, k * 4:(k + 1) * 4], start=(k == 0), stop=(k == 7)),
                              reads=[bf(f"wst{wi}"), bf("sc")], writes=[psAb], inc=(k == 7))
                else:
                    nb = blk - 4
                    for j in range(BPC):
                        pg, pgb = psG[j][nb]
                        for k in range(8):
                            E("pe", lambda e, wi=wi, j=j, k=k, pg=pg: e.matmul(
                                out=pg[:, :], lhsT=screp[j][:, k, :], rhs=wst[wi][:, k, :], start=(k == 0), stop=(k == 7)),
                              reads=[bf(f"wst{wi}"), bf(f"screp{j}")], writes=[pgb], inc=(k == 7))
                if blk + 2 < 6:
                    load_wmod(blk + 2)
            E("dve", lambda e: e.tensor_tensor(out=view(mT, 0, [[4, 16], [1, 4]]), in0=view(psA, 0, [[4, 16], [1, 4]]),
                                               in1=view(bmodT, 0, [[1, 16], [0, 4]]), op=ALU.add),
              reads=[psAb, bf("bmodT")], writes=[bf("mT")])
            E("dve", lambda e: e.tensor_copy(out=sh[:], in_=mT[:, 0:32]), reads=[bf("mT")], writes=[bf("sh")])
            E("dve", lambda e: e.scalar_tensor_tensor(out=view(gs, 0, [[4, 8], [1, 4]]), in0=view(mT, 32, [[4, 8], [1, 4]]), scalar=1.0,
                                                      in1=view(gpreT, 0, [[1, 8], [0, 4]]), op0=ALU.add, op1=ALU.mult),
              reads=[bf("mT"), bf("gpreT")], writes=[bf("gs")])
            E("dve", lambda e: e.tensor_scalar_mul(out=gs[:], in0=gs[:], scalar1=32.0), reads=[bf("gs")], writes=[bf("gs")])
            E("dve", lambda e: e.tensor_scalar_mul(out=gpost[:], in0=gpost[:], scalar1=32.0), reads=[bf("gpost")], writes=[bf("gpost")])
            for j in range(BPC):
                for nb in range(2):
                    pg, pgb = psG[j][nb]
                    E("dve", lambda e, pg=pg, nb=nb: e.tensor_tensor(out=gtmp[:], in0=pg[:, :], in1=bgate[:, nb * 512:(nb + 1) * 512], op=ALU.add),
                      reads=[pgb, bf("bgate")], writes=[bf("gtmp")])
                    E("pool", lambda e, j=j, nb=nb: e.tensor_tensor(out=ggt[j][:, nb * 512:(nb + 1) * 512], in0=gtmp[:],
                                                                 in1=gpost[:, nb * 512:(nb + 1) * 512], op=ALU.mult),
                      reads=[bf("gtmp"), bf("gpost")], writes=[bf("ggt")])
                E("sp", lambda e, j=j: e.dma_start(out=ggs_d[j:j + 1, :], in_=ggt[j][0:1, :]), reads=[bf("ggt")], writes=[bf("ggs")],
                  dma_sem=dsem(f"d_ggs{j}"))
            tr.barrier()

        NX = 3
        xs_ = [sb(f"xt{i}", [128, D], F32) for i in range(NX)]
        ropet = [sb(f"ropet{i}", [128, 128], F32) for i in range(NX)]
        xsb1 = sb("xsb", [128, D], BF16)
        xsb2 = [xsb1, xsb1]
        hT = [sb(f"hT{i}", [128, 8, 128], BF16) for i in range(3)]
        qrot = sb("qrot", [128, D], BF16)
        yb = qrot
        krot = sb("krot", [128, 256], BF16)
        rt1 = sb("rt1", [128, 512], F32)
        rt2 = sb("rt2", [128, 512], F32)
        qT = [sb(f"qT{i}", [128, 8, 128], BF16) for i in range(2)]
        NR = 4
        kT = [sb(f"kT{i}", [128, 2, 128], BF16) for i in range(NR)]
        Vr = [sb(f"Vr{i}", [128, 4, 66], BF16) for i in range(NR)]
        xa = [sb(f"xa{i}", [128, 512], BF16) for i in range(NR)]
        ttmp = sb("ttmp", [128, 512], BF16)
        Gb = sb("Gb", [128, D], BF16)
        Ga = sb("Ga", [128, 512], BF16)
        Tm = sb("Tm", [128, 2, 512], BF16)
        PT2 = [sb(f"PT{i}", [128, 5, 512], BF16) for i in range(2)]
        pooled = sb("pooled", [128, 512], BF16)
        ybT = sb("ybT", [128, 8, 128], BF16)
        mgT = sb("mgT", [128, 8, 128], BF16)
        mu = sb("mu", [128, 512], F32)
        mv = sb("mv", [128, 512], F32)
        small = sb("small", [128, 32], F32)


        for i in range(NR):
            E("pool", lambda e, i=i: e.memset(Vr[i][:, :, 64:66], 1.0), writes=[bf(f"Vr{i}")])
        for b in range(BPC):
            E("pool", lambda e, b=b: e.memset(vc[b][:, :, :, 64:66], 1.0), writes=[bf("vc")])

        c_eps1 = float(D * EPS)
        c_eps2 = float(D * 16.0 * EPS)

        def load_x(slot, src_ap, rope_idx=None):
            s = dsem(f"d_x{slot}")
            E("sp", lambda e: e.dma_start(out=xs_[slot][:], in_=src_ap), writes=[bf(f"xt{slot}")], dma_sem=s)
            if rope_idx is not None:
                s2 = dsem(f"d_rope{slot}")
                E("sp", lambda e: e.dma_start(out=ropet[slot][:], in_=rope_d[rope_idx]), writes=[bf(f"ropet{slot}")], dma_sem=s2)

        def norm_front(slot, hslot):
            xt = xs_[slot]
            xb = bf(f"xt{slot}")
            xsb = xsb2[hslot]
            xsbb = bf("xsb")
            E("act", lambda e: e.activation(out=xsb[:], in_=xt[:], func=AF.Square, accum_out=small[:, 0:1]),
              reads=[xb], writes=[xsbb, bf("ss")])
            E("dve", lambda e: e.tensor_scalar_add(out=small[:, 1:2], in0=small[:, 0:1], scalar1=c_eps1), reads=[bf("ss")], writes=[bf("ssb")])
            E("pool", lambda e: e.tensor_tensor(out=small[:, 2:3], in0=small[:, 1:2], in1=mh[:], op=ALU.pow),
              reads=[bf("ssb"), bf("mh")], writes=[bf("rstd")])
            E("act", lambda e: e.activation(out=xsb[:], in_=xt[:], func=AF.Copy, scale=small[:, 2:3]),
              reads=[xb, bf("rstd")], writes=[xsbb])

        def norm_back(hslot, h3, j):
            xsb = xsb2[hslot]
            xsbb = bf("xsb")
            tp, tpb = next_tbank()
            for k in range(8):
                E("pe", lambda e, k=k: e.transpose(out=tp[:, k * 128:(k + 1) * 128], in_=xsb[:, k * 128:(k + 1) * 128], identity=ident[:]),
                  reads=[xsbb, bf("ident")], writes=[tpb], inc=(k == 7))
            for k in range(8):
                hb = bf(f"hT{h3}_{k}")
                E("dve", lambda e, k=k: e.tensor_scalar(out=hT[h3][:, k, :], in0=tp[:, k * 128:(k + 1) * 128],
                                                        scalar1=gs[:, k * 4 + j:k * 4 + j + 1], scalar2=sh[:, k * 4 + j:k * 4 + j + 1],
                                                        op0=ALU.mult, op1=ALU.add),
                  reads=[tpb, bf("gs"), bf("sh")], writes=[hb])

        def norm_to_hT(slot, hslot, h3, j):
            norm_front(slot, hslot)
            norm_back(hslot, h3, j)

        def hT_bufs(h3):
            return [bf(f"hT{h3}_{k}") for k in range(8)]

        def proj_tok(h3, nb):
            pb, pbb = next_bank()
            for k in range(8):
                E("pe", lambda e, k=k: e.matmul(out=pb[:, :], lhsT=hT[h3][:, k, :], rhs=Wtok[:, k, nb * 512:(nb + 1) * 512],
                                                start=(k == 0), stop=(k == 7)),
                  reads=[bf(f"hT{h3}_{k}"), bf("Wtok")], writes=[pbb], inc=(k == 7))
            return pb, pbb

        def rope(pb, pbb, nheads, rslot, out_t, out_off, out_buf, add_eng="pool"):
            n = nheads * 64
            rb = bf(f"ropet{rslot}")
            cc = view(ropet[rslot], 0, [[0, nheads], [1, 64]])
            qv = view(pb, 0, [[64, nheads], [1, 64]])
            E("dve", lambda e: e.tensor_tensor(out=view(rt1, 0, [[64, nheads], [1, 64]]), in0=qv, in1=cc, op=ALU.mult),
              reads=[pbb, rb], writes=[bf("rt1")])
            for a in range(2):
                E("dve", lambda e, a=a: e.tensor_tensor(out=view(rt2, 32 * a, [[64, nheads], [16, 2], [1, 16]]),
                                                        in0=view(pb, 32 * a + 16, [[64, nheads], [-16, 2], [1, 16]]),
                                                        in1=view(ropet[rslot], 64 + 32 * a, [[0, nheads], [16, 2], [1, 16]]), op=ALU.mult),
                  reads=[pbb, rb], writes=[bf(f"rt2_{a}")])
            E(add_eng, lambda e: e.tensor_tensor(out=out_t[:, out_off:out_off + n], in0=rt1[:, 0:n], in1=rt2[:, 0:n], op=ALU.add),
              reads=[bf("rt1"), bf("rt2_0"), bf("rt2_1")], writes=[out_buf])

        def stage1(b, i, slot, hslot, h3, rslot, do_back=True):
            if do_back:
                norm_back(hslot, h3, b)
            ring = i % NR
            pb, pbb = proj_tok(h3, 2)
            rope(pb, pbb, 4, rslot, krot, 0, bf("krot"), add_eng="dve")
            E("act", lambda e: e.activation(out=Vr[ring][:, :, 0:64], in_=view(pb, 256, [[64, 4], [1, 64]]), func=AF.Copy),
              reads=[pbb], writes=[bf(f"Vr{ring}")])
            pb2, pbb2 = proj_tok(h3, 3)
            E("act", lambda e: e.activation(out=xa[ring][:], in_=pb2[:, :], func=AF.Copy), reads=[pbb2], writes=[bf(f"xa{ring}")])
            for nb in range(2):
                pbq, pbbq = proj_tok(h3, nb)
                rope(pbq, pbbq, 8, rslot, qrot, nb * 512, bf(f"qrot{nb}"))
            tp2, tpb2 = next_tbank()
            for c in range(2):
                E("pe", lambda e, c=c: e.transpose(out=tp2[:, c * 128:(c + 1) * 128], in_=krot[:, c * 128:(c + 1) * 128], identity=ident[:]),
                  reads=[bf("krot"), bf("ident")], writes=[tpb2], inc=(c == 1))
            E("dve", lambda e: e.tensor_copy(out=view(kT[ring], 0, [[1, 256]]), in_=tp2[:, 0:256]), reads=[tpb2], writes=[bf(f"kT{ring}")])

        def stage1_qT(hslot):
            tp, tpb = next_tbank()
            for c in range(8):
                E("pe", lambda e, c=c: e.transpose(out=tp[:, c * 128:(c + 1) * 128], in_=qrot[:, c * 128:(c + 1) * 128], identity=ident[:]),
                  reads=[bf(f"qrot{c // 4}"), bf("ident")], writes=[tpb], inc=(c == 7))
            E("dve", lambda e: e.tensor_copy(out=view(qT[hslot], 0, [[1, 1024]]), in_=tp[:, :]), reads=[tpb], writes=[bf(f"qT{hslot}")])

        def stage_ctx(b, t, slot, hslot, h3):
            pb, pbb = proj_tok(h3, 2)
            E("act", lambda e: e.activation(out=krot[:], in_=pb[:, 0:256], func=AF.Copy), reads=[pbb], writes=[bf("krot")])
            E("act", lambda e: e.activation(out=vc[b][:, t, :, 0:64], in_=view(pb, 256, [[64, 4], [1, 64]]), func=AF.Copy),
              reads=[pbb], writes=[bf("vc")])
            tp2, tpb2 = next_tbank()
            for c in range(2):
                E("pe", lambda e, c=c: e.transpose(out=tp2[:, c * 128:(c + 1) * 128], in_=krot[:, c * 128:(c + 1) * 128], identity=ident[:]),
                  reads=[bf("krot"), bf("ident")], writes=[tpb2], inc=(c == 1))
            E("dve", lambda e: e.tensor_copy(out=kTc[b][:, :, t * 128:(t + 1) * 128], in_=view(tp2, 0, [[128, 2], [1, 128]])),
              reads=[tpb2], writes=[bf("kTc")])

        def stage2(b, i, slot, hslot, h3, mid_hook=None, late_hook=None, early_hook=None):
            ring = i % NR
            hbufs = hT_bufs(h3)

            def feat_group(ch0, evac):
                pb, pbb = next_bank()
                for c4 in range(4):
                    ch = ch0 + c4
                    for k in range(8):
                        E("pe", lambda e, pb=pb, c4=c4, ch=ch, k=k: e.matmul(
                            out=pb[:, c4 * 128:(c4 + 1) * 128], lhsT=Wfeat[:, k, ch * 128:(ch + 1) * 128], rhs=hT[h3][:, k, :],
                            start=(k == 0), stop=(k == 7)),
                          reads=[hbufs[k], bf("Wfeat")], writes=[pbb], inc=(k == 7 and c4 == 3))
                evac(pb, pbb)

            def do_gb(nbs=(0, 1)):
                for nb in nbs:
                    pb, pbb = proj_tok(h3, 4 + nb)
                    E("act", lambda e, pb=pb: e.activation(out=ttmp[:], in_=pb[:, :], func=AF.Tanh, scale=0.5), reads=[pbb], writes=[bf("ttmp")])
                    E("dve", lambda e, pb=pb, nb=nb: e.scalar_tensor_tensor(out=Gb[:, nb * 512:(nb + 1) * 512], in0=ttmp[:], scalar=1.0, in1=pb[:, :],
                                                                           op0=ALU.add, op1=ALU.mult),
                      reads=[bf("ttmp"), pbb], writes=[bf(f"Gb{nb}")])

            def do_ga():
                def ev(pb, pbb):
                    E("act", lambda e: e.activation(out=ttmp[:], in_=pb[:, :], func=AF.Tanh, scale=0.5), reads=[pbb], writes=[bf("ttmp")])
                    E("dve", lambda e: e.scalar_tensor_tensor(out=Ga[:], in0=ttmp[:], scalar=1.0, in1=pb[:, :], op0=ALU.add, op1=ALU.mult),
                      reads=[bf("ttmp"), pbb], writes=[bf("Ga")])
                feat_group(0, ev)

            def do_pool1():
                pb, pbb = next_bank()
                for g in range(4):
                    terms = []
                    if i > 0:
                        terms.append(((i - 1) % NR, 5 * g + 3))
                    terms.append((ring, 5 * g + (1 if i == 0 else (2 if i == NT - 1 else 0))))
                    if i < NT - 1:
                        terms.append(((i + 1) % NR, 5 * g + 4))
                    for ti, (rs, mi) in enumerate(terms):
                        E("pe", lambda e, g=g, rs=rs, mi=mi, ti=ti, nterm=len(terms): e.matmul(
                            out=pb[:, g * 128:(g + 1) * 128], lhsT=xa[rs][:, g * 128:(g + 1) * 128], rhs=band[:, mi, :],
                            start=(ti == 0), stop=(ti == nterm - 1)),
                          reads=[bf(f"xa{rs}"), bf("band")], writes=[pbb], inc=(g == 3 and ti == len(terms) - 1))
                E("dve", lambda e: e.tensor_copy(out=pooled[:], in_=pb[:, :]), reads=[pbb], writes=[bf("pooled")])

            def do_pool2():
                pb2, pbb2 = next_bank()
                for g in range(4):
                    E("pe", lambda e, g=g: e.matmul(out=pb2[:, g * 128:(g + 1) * 128], lhsT=Wp[:, g, :], rhs=pooled[:, g * 128:(g + 1) * 128],
                                                    start=True, stop=True),
                      reads=[bf("pooled"), bf("Wp")], writes=[pbb2], inc=(g == 3))
                E("dve", lambda e: e.tensor_tensor(out=pooled[:], in0=pb2[:, :], in1=Ga[:], op=ALU.mult),
                  reads=[pbb2, bf("Ga")], writes=[bf("pooled")])

            blocks = []
            if i > 0:
                blocks.append(("lat", (i - 1) % NR, 0))
            blocks.append(("lat", ring, None))
            if i < NT - 1:
                blocks.append(("lat", (i + 1) % NR, 1))
            blocks.append(("ctx", 0, None))
            blocks.append(("ctx", 1, None))
            nblk = len(blocks)

            def do_qk(gs_, jbs):
                for jb in jbs:
                    kind, idx, mk = blocks[jb]
                    banks = []
                    for g in gs_:
                        p0 = 64 * (g % 2)
                        cq = 4 * (g // 2)
                        ck = g // 2
                        rhs_q = pview(qT[hslot], p0, 64, cq * 128, [[128, 4], [1, 128]])
                        sbk, sbb = next_bank()
                        banks.append((g, sbk, sbb))
                        if kind == "lat":
                            lk = pview(kT[idx], p0, 64, ck * 128, [[1, 128]])
                            kb = bf(f"kT{idx}")
                        else:
                            lk = pview(kTc[b], p0, 64, ck * CTX + idx * 128, [[1, 128]])
                            kb = bf("kTc")
                        E("pe", lambda e, sbk=sbk, lk=lk, mk=mk, rhs_q=rhs_q: e.matmul(out=sbk[:, :], lhsT=lk, rhs=rhs_q, start=True, stop=(mk is None)),
                          reads=[kb, bf(f"qT{hslot}")], writes=[sbb], inc=(mk is None))
                    for g, sbk, sbb in banks:
                        if mk is not None:
                            for r in range(4):
                                E("pe", lambda e, sbk=sbk, mk=mk, r=r: e.matmul(out=sbk[:, r * 128:(r + 1) * 128], lhsT=ident[:], rhs=maskb[:, mk, :],
                                                                               start=False, stop=(r == 3)),
                                  reads=[bf("ident"), bf("maskb")], writes=[sbb], inc=(r == 3))
                        E("act", lambda e, sbk=sbk, jb=jb, g=g: e.activation(out=PT2[g % 2][:, jb, :], in_=sbk[:, :], func=AF.Exp, scale=0.125),
                          reads=[sbb], writes=[bf(f"PT{g % 2}_{jb}")])

            def do_pv(g):
                ob, obb = next_bank()
                for r in range(4):
                    for jb, (kind, idx, mk) in enumerate(blocks):
                        if kind == "lat":
                            rv = pview(Vr[idx], 0, 128, g * 66, [[1, 65]])
                            vb_ = bf(f"Vr{idx}")
                        else:
                            rv = pview(vc[b], 0, 128, (idx * 4 + g) * 66, [[1, 65]])
                            vb_ = bf("vc")
                        E("pe", lambda e, r=r, jb=jb, rv=rv: e.matmul(out=ob[:, r * 128:r * 128 + 65], lhsT=PT2[g % 2][:, jb, r * 128:(r + 1) * 128], rhs=rv,
                                                                     start=(jb == 0), stop=(jb == nblk - 1)),
                          reads=[bf(f"PT{g % 2}_{jb}"), vb_], writes=[obb], inc=(r == 3 and jb == nblk - 1))
                hoff = 8 * (g // 2) + (g % 2)
                E("dve", lambda e: e.tensor_tensor(out=small[:, 8:12], in0=view(ob, 64, [[128, 4]]),
                                                   in1=view(esink, hoff, [[2, 4]]), op=ALU.add),
                  reads=[obb, bf("esink")], writes=[bf("den")])
                E("dve", lambda e: e.reciprocal(out=small[:, 12:16], in_=small[:, 8:12]), reads=[bf("den")], writes=[bf("rec")])
                for r in range(4):
                    co = (hoff + 2 * r) * 64
                    extra = [bf("qrot0"), bf("qrot1")] if (g == 0 and r == 0) else []
                    E("dve", lambda e, r=r, co=co: e.scalar_tensor_tensor(out=yb[:, co:co + 64], in0=ob[:, r * 128:r * 128 + 64],
                                                                         scalar=small[:, 12 + r:13 + r], in1=Gb[:, co:co + 64],
                                                                         op0=ALU.mult, op1=ALU.mult),
                      reads=[obb, bf("rec"), bf("Gb0"), bf("Gb1")], writes=[bf(f"yb{g}_{r}")] + extra)

            def do_ybT():
                tp, tpb = next_tbank()
                for c in range(8):
                    E("pe", lambda e, c=c: e.transpose(out=tp[:, c * 128:(c + 1) * 128], in_=yb[:, c * 128:(c + 1) * 128], identity=ident[:]),
                      reads=[bf(f"yb{g}_{r}") for g in range(4) for r in range(4)] + [bf("qrot0"), bf("qrot1"), bf("ident")], writes=[tpb], inc=(c == 7))
                E("dve", lambda e: e.tensor_copy(out=view(ybT, 0, [[1, 1024]]), in_=tp[:, :]), reads=[tpb], writes=[bf("ybT")])

            def do_A(rd, c4s=(0, 1, 2, 3), bank=None):
                pa, pab = bank if bank is not None else next_bank()
                for c4 in c4s:
                    cd = rd * 4 + c4
                    for k in range(4):
                        E("pe", lambda e, c4=c4, cd=cd, k=k: e.matmul(out=pa[:, c4 * 128:(c4 + 1) * 128], lhsT=Wa[:, k, cd * 128:(cd + 1) * 128],
                                                                     rhs=pooled[:, k * 128:(k + 1) * 128], start=(k == 0), stop=(k == 3)),
                          reads=[bf("pooled"), bf("Wa")], writes=[pab], inc=(k == 3 and c4 == 3))
                return pa, pab

            def do_B(rd):
                pbk, pbkb = next_bank()
                for c4 in range(4):
                    cd = rd * 4 + c4
                    for k in range(8):
                        E("pe", lambda e, c4=c4, cd=cd, k=k: e.matmul(out=pbk[:, c4 * 128:(c4 + 1) * 128], lhsT=Wb[:, k, cd * 128:(cd + 1) * 128],
                                                                     rhs=ybT[:, k, :], start=(k == 0), stop=(k == 7)),
                          reads=[bf("ybT"), bf("Wb")], writes=[pbkb], inc=(k == 7 and c4 == 3))
                return pbk, pbkb

            def do_gm(rd, halves=(0, 1)):
                for half in halves:
                    def ev(pgm, pgmb, half=half):
                        E("act", lambda e: e.activation(out=Tm[:, half, :], in_=pgm[:, :], func=AF.Tanh, scale=0.5),
                          reads=[pgmb], writes=[bf(f"Tm{half}")])
                    feat_group(4 + half * 8 + rd * 4, ev)

            def do_u(pa, pab):
                E("dve", lambda e: e.scalar_tensor_tensor(out=mu[:], in0=Tm[:, 0, :], scalar=1.0, in1=pa[:, :], op0=ALU.add, op1=ALU.mult),
                  reads=[bf("Tm0"), pab], writes=[bf("mu")])

            def do_v_merge(rd, pbk, pbkb):
                E("dve", lambda e: e.scalar_tensor_tensor(out=mv[:], in0=Tm[:, 1, :], scalar=1.0, in1=pbk[:, :], op0=ALU.add, op1=ALU.mult),
                  reads=[bf("Tm1"), pbkb], writes=[bf("mv")])
                E("pool", lambda e: e.tensor_tensor(out=view(mgT, rd * 512, [[1, 512]]), in0=mu[:], in1=mv[:], op=ALU.add),
                  reads=[bf("mu"), bf("mv")], writes=[bf(f"mgT{rd}")])

            def do_out():
                obs = [next_bank() for nb in range(2)]
                for kh in range(2):
                    for nb in range(2):
                        pb, pbb = obs[nb]
                        for k in range(4 * kh, 4 * kh + 4):
                            E("pe", lambda e, pb=pb, nb=nb, k=k: e.matmul(out=pb[:, :], lhsT=mgT[:, k, :], rhs=Wo[:, k, nb * 512:(nb + 1) * 512],
                                                                         start=(k == 0), stop=(k == 7)),
                              reads=[bf(f"mgT{k // 4}"), bf("Wo")], writes=[pbb], inc=(k == 7))
                for nb in range(2):
                    pb, pbb = obs[nb]
                    E("act", lambda e, pb=pb, nb=nb: e.activation(out=ttmp[:], in_=pb[:, :], func=AF.Square, accum_out=small[:, 3 + nb:4 + nb]),
                      reads=[pbb], writes=[bf("ttmp"), bf(f"ss2_{nb}")])
                E("dve", lambda e: e.tensor_tensor(out=small[:, 7:8], in0=small[:, 3:4], in1=small[:, 4:5], op=ALU.add),
                  reads=[bf("ss2_0"), bf("ss2_1")], writes=[bf("ss2t")])
                E("dve", lambda e: e.tensor_scalar_add(out=small[:, 5:6], in0=small[:, 7:8], scalar1=c_eps2),
                  reads=[bf("ss2t")], writes=[bf("ss2s")])
                E("pool", lambda e: e.tensor_tensor(out=small[:, 6:7], in0=small[:, 5:6], in1=mh[:], op=ALU.pow),
                  reads=[bf("ss2s"), bf("mh")], writes=[bf("r2")])
                for nb in range(2):
                    pb, pbb = obs[nb]
                    E("dve", lambda e, pb=pb, nb=nb: e.scalar_tensor_tensor(out=mu[:], in0=pb[:, :], scalar=small[:, 6:7], in1=gg[:, nb * 512:(nb + 1) * 512],
                                                                           op0=ALU.mult, op1=ALU.mult),
                      reads=[pbb, bf("r2"), bf("gg")], writes=[bf("mu")])
                    E("pool", lambda e, nb=nb: e.tensor_tensor(out=xs_[slot][:, nb * 512:(nb + 1) * 512], in0=xs_[slot][:, nb * 512:(nb + 1) * 512],
                                                               in1=mu[:], op=ALU.add),
                      reads=[bf("mu"), bf(f"xt{slot}")], writes=[bf(f"xt{slot}")])
                s_ = dsem(f"d_y{slot}")
                E("sp", lambda e: e.dma_start(out=y_d[b, i * 128:(i + 1) * 128, :], in_=xs_[slot][:]), reads=[bf(f"xt{slot}")], writes=[bf(f"xt{slot}")],
                  dma_sem=s_)

            jall = list(range(nblk))
            c1, c2, c3 = jall[0:2], jall[2:4], jall[4:]
            do_gb((0,))
            do_qk((0, 1), c1)
            do_gb((1,))
            do_qk((0, 1), c2)
            do_qk((0, 1), c3)
            do_ga()
            if early_hook is not None:
                early_hook()
            do_pool1()
            do_pv(0)
            do_pv(1)
            do_qk((2, 3), c1)
            do_pool2()
            do_gm(0, (0,))
            do_qk((2, 3), c2)
            do_qk((2, 3), c3)
            if mid_hook is not None:
                mid_hook()
            pa0 = do_A(0)
            do_gm(0, (1,))
            do_u(*pa0)
            do_pv(2)
            do_pv(3)
            do_gm(1, (0,))
            pa1 = do_A(1, (0, 1))
            do_ybT()
            pa1 = do_A(1, (2, 3), bank=pa1)
            pb0 = do_B(0)
            do_v_merge(0, *pb0)
            do_gm(1, (1,))
            pb1 = do_B(1)
            do_u(*pa1)
            do_v_merge(1, *pb1)
            if late_hook is not None:
                late_hook()
            do_out()

        work = []
        for b in range(BPC):
            for t in range(2):
                work.append(("ctx", b, t))
            for i in range(NT):
                work.append(("lat", b, i))
        gidx = {w: n for n, w in enumerate(work)}

        def issue_load(w):
            kind, b, i = w
            slot = gidx[w] % NX
            if kind == "ctx":
                load_x(slot, ctx_d[b, i * 128:(i + 1) * 128, :])
            else:
                load_x(slot, x_d[b, i * 128:(i + 1) * 128, :], rope_idx=i)

        issue_load(work[0])
        front_done = set()
        back_done = set()

        def front(w):
            if w is None or w in front_done:
                return
            front_done.add(w)
            norm_front(gidx[w] % NX, gidx[w] % 2)

        def back(w):
            if w is None or w in back_done:
                return
            front(w)
            back_done.add(w)
            norm_back(gidx[w] % 2, gidx[w] % 3, (2 if w[0] == "ctx" else w[1]))

        for n, w in enumerate(work):
            kind, b, i = w
            if n + 1 < len(work):
                issue_load(work[n + 1])
            slot = gidx[w] % NX
            hslot = gidx[w] % 2
            h3 = gidx[w] % 3
            if kind == "ctx":
                if i == 0:
                    E("sp", lambda e, b=b: e.dma_start(out=gg[:], in_=ggs_d[b:b + 1, :].partition_broadcast(128)),
                      reads=[bf("ggs")], writes=[bf("gg")], dma_sem=dsem("d_gg"))
                back(w)
                front(work[n + 1])
                stage_ctx(b, i, slot, hslot, h3)
                continue
            back(w)
            nxt = work[n + 1] if n + 1 < len(work) else None
            if i == 0:
                front(nxt)
            stage1(b, i, slot, hslot, h3, slot, do_back=False)
            if nxt is not None and nxt[0] != "lat":
                nxt = None
            qt_hook = (lambda hslot=hslot: stage1_qT(hslot))
            if i > 0:
                wp = ("lat", b, i - 1)
                stage2(b, i - 1, gidx[wp] % NX, gidx[wp] % 2, gidx[wp] % 3,
                       mid_hook=(lambda nxt=nxt: front(nxt)), late_hook=(lambda nxt=nxt: back(nxt)), early_hook=qt_hook)
            else:
                stage1_qT(hslot)
            if i == NT - 1:
                stage2(b, i, slot, hslot, h3)

        tr.final_wait("sp", [k for k in dma_sems if k.startswith("d_y")])
        print("sbuf remaining", nc.sbuf_bytes_remaining, {e: len(tr.ops[e]) for e in ENGINES})

        sems = {}
        for k in list(ENGINES) + dma_sems:
            sems[k] = es.enter_context(nc.semaphore(k))
        with nc.Block() as block:
            @block.sync
            def _(e):
                tr.replay_engine("sp", e, sems)

            @block.scalar
            def _(e):
                tr.replay_engine("act", e, sems)

            @block.vector
            def _(e):
                tr.replay_engine("dve", e, sems)

            @block.gpsimd
            def _(e):
                tr.replay_engine("pool", e, sems)

            @block.tensor
            def _(e):
                tr.replay_engine("pe", e, sems)
    return nc


def _prep_shared(w_mod, b_mod, norm_pre_g, norm_post_g, w_in, pool_w, pool_scale, sink, w_branch_a, w_branch_b, w_out, L):
    f = lambda a: np.ascontiguousarray(np.asarray(a, dtype=np.float32))
    w_in = f(w_in)[0]
    hp = np.array(HP)
    colperm = (hp[:, None] * 64 + np.arange(64)[None, :]).reshape(-1)
    xa_c = w_in[:, 0:512]
    ga_c = w_in[:, 512:1024]
    q_c = w_in[:, 1024:2048][:, colperm]
    k_c = w_in[:, 2048:2304]
    v_c = w_in[:, 2304:2560]
    gb_c = w_in[:, 2560:3584][:, colperm]
    gm_c = w_in[:, 3584:5632]
    w_tok = f(np.concatenate([q_c, k_c, v_c, xa_c, gb_c], axis=1))
    w_feat = f(np.concatenate([ga_c, gm_c], axis=1))
    b_mod = f(b_mod)[0]
    shared = {
        "w_tok": w_tok,
        "w_feat": w_feat,
        "w_a": f(w_branch_a)[0],
        "w_b": f(f(w_branch_b)[0][colperm, :]),
        "w_o": f(w_out)[0],
        "pool_w": f(f(pool_w)[0].transpose(1, 0, 2).reshape(128, 512)),
        "w_mod": f(w_mod)[0],
        "b_modT": f(b_mod[0:2048].reshape(16, 128).T),
        "b_gate": f(b_mod[2048:3072].reshape(1, D)),
        "g_preT": f(f(norm_pre_g)[0].reshape(8, 128).T),
        "g_post": f(f(norm_post_g)[0].reshape(1, D)),
        "pool_scale": f(f(pool_scale)[0].reshape(1, 512)),
        "sink": f(f(sink)[0][hp].reshape(1, 16)),
        "rope": rope_tables(L),
        "band": band_tables(),
        "mask": mask_tables(),
    }
    return shared


_PROGRAMS = {}


def kernel_impl(x, c, ctx, c_ctx, w_mod, b_mod, norm_pre_g, norm_post_g, w_in, pool_w, pool_scale,
                sink, w_branch_a, w_branch_b, w_out):
    x = np.asarray(x, dtype=np.float32)
    c = np.asarray(c, dtype=np.float32)
    ctx = np.asarray(ctx, dtype=np.float32)
    c_ctx = np.asarray(c_ctx, dtype=np.float32)
    Bn, L, _ = x.shape
    assert Bn == NCORES * BPC
    shared = _prep_shared(w_mod, b_mod, norm_pre_g, norm_post_g, w_in, pool_w, pool_scale, sink,
                          w_branch_a, w_branch_b, w_out, L)
    in_maps = []
    for core in range(NCORES):
        bs = slice(core * BPC, (core + 1) * BPC)
        c3 = np.zeros((4, D), np.float32)
        c3[0:BPC] = c[bs]
        c3[2] = c_ctx
        cT = np.ascontiguousarray(c3.reshape(4, 8, 128).transpose(2, 1, 0).reshape(128, 32))
        m = dict(shared)
        m["x"] = np.ascontiguousarray(x[bs])
        m["ctx"] = np.ascontiguousarray(ctx[bs])
        m["cT"] = cT
        in_maps.append(m)
    if L not in _PROGRAMS:
        _PROGRAMS[L] = build_program(L)
    nc = _PROGRAMS[L]
    res = run_bass_kernel_spmd(nc, in_maps, core_ids=list(range(NCORES)))
    out = np.concatenate([np.asarray(r["y"], dtype=np.float32) for r in res.results], axis=0)
    return out


def kernel(**inputs):
    return kernel_impl(**inputs)
```

```python
import numpy as np
from contextlib import ExitStack

import concourse.bass as bass
import concourse.mybir as mybir
from concourse.ap import AP
from concourse.bass_utils import run_bass_kernel_spmd

F32 = mybir.dt.float32
BF16 = mybir.dt.bfloat16
AF = mybir.ActivationFunctionType
ALU = mybir.AluOpType

D = 1024
NCORES = 8
BPC = 2
CTX = 256
EPS = 1e-6
NEG = -30000.0
HP = [0, 4, 1, 5, 2, 6, 3, 7, 8, 12, 9, 13, 10, 14, 11, 15]
POOL_WINDOWS = (2, 4, 8, 16)
ENGINES = ("pe", "act", "dve", "pool", "sp")


class Buf:
    __slots__ = ("name", "last_w", "readers", "excl", "touch")

    def __init__(self, name, excl=False):
        self.name = name
        self.last_w = None
        self.readers = []
        self.touch = 0
        self.excl = excl


class Tracker:
    def __init__(self):
        self.ops = {e: [] for e in ENGINES}
        self.count = {}
        self.known = {e: {} for e in ENGINES}
        self.snap = {}
        self.opn = 0

    def _need(self, eng, ev, waits):
        if ev is None:
            return
        key, val = ev
        if eng == "pe" and key == "pe":
            return
        if self.known[eng].get(key, 0) >= val:
            return
        if val > self.count.get(key, 0):
            raise RuntimeError(f"pending event {ev} needed by {eng}")
        waits[key] = max(waits.get(key, 0), val)

    def _apply_waits(self, eng, waits):
        kn = self.known[eng]
        for key, val in waits.items():
            if kn.get(key, 0) < val:
                kn[key] = val
            sn = self.snap.get((key, val))
            if sn:
                for k2, v2 in sn.items():
                    if kn.get(k2, 0) < v2:
                        kn[k2] = v2

    def emit(self, eng, fn, reads=(), writes=(), inc=True, dma_sem=None):
        waits = {}
        self.opn += 1
        for b in reads:
            b.touch = self.opn
        for b in writes:
            b.touch = self.opn
        for b in reads:
            self._need(eng, b.last_w, waits)
            if b.excl:
                for r in b.readers:
                    if r[0] != eng:
                        self._need(eng, r, waits)
        for b in writes:
            self._need(eng, b.last_w, waits)
            for r in b.readers:
                self._need(eng, r, waits)
        self._apply_waits(eng, waits)
        if dma_sem is not None:
            self.count[dma_sem] = self.count.get(dma_sem, 0) + 16
            ev = (dma_sem, self.count[dma_sem])
            incspec = (dma_sem, 16)
        elif inc:
            self.count[eng] = self.count.get(eng, 0) + 1
            ev = (eng, self.count[eng])
            incspec = (eng, 1)
        else:
            ev = (eng, self.count.get(eng, 0) + 1)
            incspec = None
        self.snap[ev] = dict(self.known[eng])
        for b in reads:
            b.readers.append(ev)
        for b in writes:
            b.last_w = ev
            b.readers = []
        self.ops[eng].append((sorted(waits.items()), fn, incspec))
        return ev

    def barrier(self):
        allv = dict(self.count)
        for eng in ENGINES:
            waits = {}
            for key, val in allv.items():
                if val > 0 and self.known[eng].get(key, 0) < val:
                    if eng == "pe" and key == "pe":
                        continue
                    waits[key] = val
            self._apply_waits(eng, waits)
            if waits:
                self.ops[eng].append((sorted(waits.items()), None, None))

    def final_wait(self, eng, keys):
        waits = {k: self.count[k] for k in keys if self.count.get(k, 0) > 0}
        self._apply_waits(eng, waits)
        self.ops[eng].append((sorted(waits.items()), None, None))

    def replay_engine(self, eng, eobj, sems):
        for waits, fn, incspec in self.ops[eng]:
            for key, val in waits:
                eobj.wait_ge(sems[key], val)
            if fn is None:
                continue
            ins = fn(eobj)
            if incspec is not None:
                ins.then_inc(sems[incspec[0]], incspec[1])


def rope_tables(L):
    nt = L // 128
    t = np.arange(L)
    row = (t // 64).astype(np.float64)
    col = (t % 64).astype(np.float64)
    freqs = 10000.0 ** (-np.arange(16, dtype=np.float64) / 16.0)
    out = np.zeros((L, 2, 2, 2, 16), np.float64)
    for a, pos in enumerate((row, col)):
        ang = (pos.astype(np.float32)[:, None] * freqs.astype(np.float32)[None, :]).astype(np.float32).astype(np.float64)
        out[:, 0, a, 0, :] = np.cos(ang)
        out[:, 0, a, 1, :] = np.cos(ang)
        out[:, 1, a, 0, :] = -np.sin(ang)
        out[:, 1, a, 1, :] = np.sin(ang)
    return out.reshape(nt, 128, 128).astype(np.float32)


def band_tables():
    M = np.zeros((128, 20, 128), np.float64)
    ss = np.arange(128)[:, None]
    tt = np.arange(128)[None, :]
    eye = (ss == tt).astype(np.float64)
    for g, w in enumerate(POOL_WINDOWS):
        h = w // 2
        M[:, 5 * g + 0, :] = ((ss >= tt - h) & (ss < tt + h)) / w - eye
        lo = np.maximum(tt - h, 0)
        M[:, 5 * g + 1, :] = ((ss >= lo) & (ss < tt + h)) / (tt + h - lo) - eye
        hi = np.minimum(tt + h, 128)
        M[:, 5 * g + 2, :] = ((ss >= tt - h) & (ss < hi)) / (hi - (tt - h)) - eye
        M[:, 5 * g + 3, :] = ((ss - 128 >= tt - h) & (ss - 128 < tt + h)) / w
        M[:, 5 * g + 4, :] = ((ss + 128 >= tt - h) & (ss + 128 < tt + h)) / w
    return M.reshape(128, 20 * 128).astype(np.float32)


def mask_tables():
    kk = np.arange(128)[:, None]
    qq = np.arange(128)[None, :]
    m = np.zeros((128, 2, 128), np.float32)
    m[:, 0, :] = np.where(qq <= kk, 0.0, NEG)
    m[:, 1, :] = np.where(kk <= qq, 0.0, NEG)
    return m.reshape(128, 256)


def build_program(L, STOP=0):
    NT = L // 128
    nc = bass.Bass("TRN2", target_bir_lowering=False)

    def din(name, shape):
        return nc.dram_tensor(name, list(shape), F32, kind="ExternalInput").ap()

    x_d = din("x", [BPC, L, D])
    ctx_d = din("ctx", [BPC, CTX, D])
    cT_d = din("cT", [128, 32])
    wtok_d = din("w_tok", [D, 3072])
    wfeat_d = din("w_feat", [D, 2560])
    wa_d = din("w_a", [512, D])
    wb_d = din("w_b", [D, D])
    wout_d = din("w_o", [D, D])
    poolw_d = din("pool_w", [128, 512])
    wmod_d = din("w_mod", [D, 3072])
    bmodT_d = din("b_modT", [128, 16])
    bgate_d = din("b_gate", [1, D])
    gpreT_d = din("g_preT", [128, 8])
    gpost_d = din("g_post", [1, D])
    pscale_d = din("pool_scale", [1, 512])
    sink_d = din("sink", [1, 16])
    rope_d = din("rope", [NT, 128, 128])
    band_d = din("band", [128, 2560])
    mask_d = din("mask", [128, 256])
    y_d = nc.dram_tensor("y", [BPC, L, D], F32, kind="ExternalOutput").ap()
    ggs_d = nc.dram_tensor("gg_scr", [BPC, D], F32, kind="ExternalOutput").ap()

    tr = Tracker()
    E = tr.emit
    es = ExitStack()
    semkeys = list(ENGINES)

    def pstep(t):
        return t[:].ap[0][0]

    def view(t, off, dims):
        a = t[:]
        return AP(a.tensor, a.offset + off, [[a.ap[0][0], 128]] + [list(d) for d in dims])

    def pview(t, p0, pn, off, dims):
        a = t[p0:p0 + pn]
        return AP(a.tensor, a.offset + off, [[a.ap[0][0], pn]] + [list(d) for d in dims])

    with es:
        def sb(name, shape, dt, stack=es):
            return stack.enter_context(nc.sbuf_tensor("s_" + name, list(shape), dt))

        Wtok = sb("Wtok", [128, 8, 3072], BF16)
        Wfeat = sb("Wfeat", [128, 8, 2560], BF16)
        Wa = sb("Wa", [128, 4, D], BF16)
        Wb = sb("Wb", [128, 8, D], BF16)
        Wo = sb("Wo", [128, 8, D], BF16)
        Wp = sb("Wp", [128, 4, 128], BF16)
        ident = sb("ident", [128, 128], BF16)
        band = sb("band", [128, 20, 128], BF16)
        maskb = sb("maskb", [128, 2, 128], BF16)
        mh = sb("mh", [128, 1], F32)
        gs = sb("gs", [128, 32], F32)
        sh = sb("sh", [128, 32], F32)
        gg = sb("gg", [128, D], F32)
        esink = sb("esink", [128, 16], F32)
        kTc1 = sb("kTc", [128, 2, CTX], BF16)
        vc1 = sb("vc", [128, 2, 4, 66], BF16)
        kTc = [kTc1, kTc1]
        vc = [vc1, vc1]

        NPB, NTB = 7, 1
        pbank = [es.enter_context(nc.psum_tensor(f"pb{i}", [128, 512], F32)) for i in range(NPB)]
        tbank = [es.enter_context(nc.psum_tensor(f"tb{i}", [128, 1024], BF16)) for i in range(NTB)]
        pbuf = [Buf(f"pb{i}", excl=True) for i in range(NPB)]
        tbuf = [Buf(f"tb{i}", excl=True) for i in range(NTB)]

        def _lru(bufs):
            i = min(range(len(bufs)), key=lambda j: bufs[j].touch)
            tr.opn += 1
            bufs[i].touch = tr.opn
            return i

        def next_bank():
            i = _lru(pbuf)
            return pbank[i], pbuf[i]

        def next_tbank():
            i = _lru(tbuf)
            return tbank[i], tbuf[i]

        B = {}

        def bf(name):
            if name not in B:
                B[name] = Buf(name)
            return B[name]

        dma_sems = []

        def dsem(name):
            if name not in dma_sems:
                dma_sems.append(name)
            return name

        E("pool", lambda e: e.memset(mh[:], -0.5), writes=[bf("mh")])

        def ld(dst_ap, src_ap, bname):
            s_ = dsem("d_" + bname)
            E("sp", lambda e: e.dma_start(out=dst_ap, in_=src_ap), writes=[bf(bname)], dma_sem=s_)

        with ExitStack() as ps_:
            identf = sb("identf", [128, 128], F32, ps_)
            NS = 5
            stg = [sb(f"stg{i}", [128, 1536], F32, ps_) for i in range(NS)]
            pscl = sb("pscl", [128, 512], F32, ps_)
            sinkb = sb("sinkb", [128, 16], F32, ps_)

            E("pool", lambda e: e.memset(identf[:], 0.0), writes=[bf("identf")])
            E("pool", lambda e: e.affine_select(out=identf[:], in_=identf[:], pattern=[[-1, 128]],
                                                compare_op=ALU.not_equal, fill=1.0, base=0, channel_multiplier=1),
              reads=[bf("identf")], writes=[bf("identf")])
            E("pool", lambda e: e.tensor_copy(out=ident[:], in_=identf[:]), reads=[bf("identf")], writes=[bf("ident")])
            ld(pscl[:], pscale_d.partition_broadcast(128), "pscl")
            ld(sinkb[:], sink_d.partition_broadcast(128), "sinkb")

            cast_rr = {"i": 0, "s": 0}

            def load_cast(dst_ap, src_ap, n, dst_buf, mul_ap=None):
                si = cast_rr["s"] % NS
                cast_rr["s"] += 1
                sbuf_ = bf(f"stg{si}")
                s_ = dsem(f"d_stg{si}")
                E("sp", lambda e: e.dma_start(out=stg[si][:, 0:n], in_=src_ap), writes=[sbuf_], dma_sem=s_)
                if mul_ap is not None:
                    E("dve", lambda e: e.tensor_tensor(out=dst_ap, in0=stg[si][:, 0:n], in1=mul_ap, op=ALU.mult),
                      reads=[sbuf_, bf("pscl")], writes=[dst_buf])
                    return
                ci = cast_rr["i"] % 2
                cast_rr["i"] += 1
                if ci == 0:
                    E("dve", lambda e: e.tensor_copy(out=dst_ap, in_=stg[si][:, 0:n]), reads=[sbuf_], writes=[dst_buf])
                else:
                    E("act", lambda e: e.activation(out=dst_ap, in_=stg[si][:, 0:n], func=AF.Copy), reads=[sbuf_], writes=[dst_buf])

            for k in range(8):
                for h in range(2):
                    load_cast(Wtok[:, k, h * 1536:(h + 1) * 1536], wtok_d[k * 128:(k + 1) * 128, h * 1536:(h + 1) * 1536], 1536, bf("Wtok"))
            for k in range(8):
                for h in range(2):
                    load_cast(Wfeat[:, k, h * 1280:(h + 1) * 1280], wfeat_d[k * 128:(k + 1) * 128, h * 1280:(h + 1) * 1280], 1280, bf("Wfeat"))
            for k in range(4):
                load_cast(Wa[:, k, :], wa_d[k * 128:(k + 1) * 128, :], 1024, bf("Wa"))
            for k in range(8):
                load_cast(Wb[:, k, :], wb_d[k * 128:(k + 1) * 128, :], 1024, bf("Wb"))
            for k in range(8):
                load_cast(Wo[:, k, :], wout_d[k * 128:(k + 1) * 128, :], 1024, bf("Wo"))
            load_cast(view(Wp, 0, [[1, 512]]), poolw_d, 512, bf("Wp"), mul_ap=pscl[:])
            load_cast(view(band, 0, [[1, 1536]]), band_d[:, 0:1536], 1536, bf("band"))
            load_cast(view(band, 1536, [[1, 1024]]), band_d[:, 1536:2560], 1024, bf("band"))
            load_cast(view(maskb, 0, [[1, 256]]), mask_d, 256, bf("maskb"))
            E("act", lambda e: e.activation(out=esink[:], in_=sinkb[:], func=AF.Exp), reads=[bf("sinkb")], writes=[bf("esink")])
            tr.barrier()

        with ExitStack() as ps_:
            wst = [sb(f"wst{i}", [128, 8, 512], F32, ps_) for i in range(2)]
            cT = sb("cT", [128, 32], F32, ps_)
            th = sb("th", [128, 32], F32, ps_)
            sc = sb("sc", [128, 32], F32, ps_)
            screp = [sb(f"screp{j}", [128, 8, 128], F32, ps_) for j in range(BPC)]
            mT = sb("mT", [128, 64], F32, ps_)
            bmodT = sb("bmodT", [128, 16], F32, ps_)
            gpreT = sb("gpreT", [128, 8], F32, ps_)
            bgate = sb("bgate", [128, D], F32, ps_)
            gpost = sb("gpost", [128, D], F32, ps_)
            gtmp = sb("gtmp", [128, 512], F32, ps_)
            ggt1 = sb("ggt", [128, D], F32, ps_)
            ggt = [ggt1, ggt1]

            ld(cT[:], cT_d, "cT")
            def load_wmod(blk):
                wi = blk % 2
                E("sp", lambda e: e.dma_start(
                    out=wst[wi][:], in_=wmod_d[:, blk * 512:(blk + 1) * 512].rearrange("(k p) n -> p k n", p=128)),
                  writes=[bf(f"wst{wi}")], dma_sem=dsem(f"d_wst{wi}"))
            load_wmod(0)
            load_wmod(1)
            ld(bmodT[:], bmodT_d, "bmodT")
            ld(gpreT[:], gpreT_d, "gpreT")
            ld(bgate[:], bgate_d.partition_broadcast(128), "bgate")
            ld(gpost[:], gpost_d.partition_broadcast(128), "gpost")

            E("act", lambda e: e.activation(out=th[:], in_=cT[:], func=AF.Tanh, scale=0.5), reads=[bf("cT")], writes=[bf("th")])
            E("dve", lambda e: e.scalar_tensor_tensor(out=sc[:], in0=th[:], scalar=1.0, in1=cT[:], op0=ALU.add, op1=ALU.mult),
              reads=[bf("th"), bf("cT")], writes=[bf("sc")])
            E("dve", lambda e: e.tensor_scalar_mul(out=sc[:], in0=sc[:], scalar1=0.5), reads=[bf("sc")], writes=[bf("sc")])
            for j in range(BPC):
                E("dve", lambda e, j=j: e.tensor_copy(out=screp[j][:], in_=view(sc, j, [[4, 8], [0, 128]])),
                  reads=[bf("sc")], writes=[bf(f"screp{j}")])

            psA, psAb = next_bank()
            psG = [[next_bank() for nb in range(2)] for j in range(BPC)]
            for blk in range(6):
                wi = blk % 2
                if blk < 4:
                    for cc in range(4):
                        ch = blk * 4 + cc
                        for k in range(8):
                            E("pe", lambda e, wi=wi, cc=cc, ch=ch, k=k: e.matmul(
                                out=psA[:, ch * 4:(ch + 1) * 4], lhsT=wst[wi][:, k, cc * 128:(cc + 1) * 128],
                                rhs=sc[:, k * 4:(k + 1) * 4], start=(k == 0), stop=(k == 7)),
                              reads=[bf(f"wst{wi}"), bf("sc")], writes=[psAb], inc=(k == 7))
                else:
                    nb = blk - 4
                    for j in range(BPC):
                        pg, pgb = psG[j][nb]
                        for k in range(8):
                            E("pe", lambda e, wi=wi, j=j, k=k, pg=pg: e.matmul(
                                out=pg[:, :], lhsT=screp[j][:, k, :], rhs=wst[wi][:, k, :], start=(k == 0), stop=(k == 7)),
                              reads=[bf(f"wst{wi}"), bf(f"screp{j}")], writes=[pgb], inc=(k == 7))
                if blk + 2 < 6:
                    load_wmod(blk + 2)
            E("dve", lambda e: e.tensor_tensor(out=view(mT, 0, [[4, 16], [1, 4]]), in0=view(psA, 0, [[4, 16], [1, 4]]),
                                               in1=view(bmodT, 0, [[1, 16], [0, 4]]), op=ALU.add),
              reads=[psAb, bf("bmodT")], writes=[bf("mT")])
            E("dve", lambda e: e.tensor_copy(out=sh[:], in_=mT[:, 0:32]), reads=[bf("mT")], writes=[bf("sh")])
            E("dve", lambda e: e.scalar_tensor_tensor(out=view(gs, 0, [[4, 8], [1, 4]]), in0=view(mT, 32, [[4, 8], [1, 4]]), scalar=1.0,
                                                      in1=view(gpreT, 0, [[1, 8], [0, 4]]), op0=ALU.add, op1=ALU.mult),
              reads=[bf("mT"), bf("gpreT")], writes=[bf("gs")])
            E("dve", lambda e: e.tensor_scalar_mul(out=gs[:], in0=gs[:], scalar1=32.0), reads=[bf("gs")], writes=[bf("gs")])
            E("dve", lambda e: e.tensor_scalar_mul(out=gpost[:], in0=gpost[:], scalar1=32.0), reads=[bf("gpost")], writes=[bf("gpost")])
            for j in range(BPC):
                for nb in range(2):
                    pg, pgb = psG[j][nb]
                    E("dve", lambda e, pg=pg, nb=nb: e.tensor_tensor(out=gtmp[:], in0=pg[:, :], in1=bgate[:, nb * 512:(nb + 1) * 512], op=ALU.add),
                      reads=[pgb, bf("bgate")], writes=[bf("gtmp")])
                    E("pool", lambda e, j=j, nb=nb: e.tensor_tensor(out=ggt[j][:, nb * 512:(nb + 1) * 512], in0=gtmp[:],
                                                                 in1=gpost[:, nb * 512:(nb + 1) * 512], op=ALU.mult),
                      reads=[bf("gtmp"), bf("gpost")], writes=[bf("ggt")])
                E("sp", lambda e, j=j: e.dma_start(out=ggs_d[j:j + 1, :], in_=ggt[j][0:1, :]), reads=[bf("ggt")], writes=[bf("ggs")],
                  dma_sem=dsem(f"d_ggs{j}"))
            tr.barrier()

        NX = 3
        xs_ = [sb(f"xt{i}", [128, D], F32) for i in range(NX)]
        ropet = [sb(f"ropet{i}", [128, 128], F32) for i in range(NX)]
        xsb1 = sb("xsb", [128, D], BF16)
        xsb2 = [xsb1, xsb1]
        hT = [sb(f"hT{i}", [128, 8, 128], BF16) for i in range(3)]
        qrot = sb("qrot", [128, D], BF16)
        yb = qrot
        krot = sb("krot", [128, 256], BF16)
        rt1 = sb("rt1", [128, 512], F32)
        rt2 = sb("rt2", [128, 512], F32)
        qT = [sb(f"qT{i}", [128, 8, 128], BF16) for i in range(2)]
        NR = 4
        kT = [sb(f"kT{i}", [128, 2, 128], BF16) for i in range(NR)]
        Vr = [sb(f"Vr{i}", [128, 4, 66], BF16) for i in range(NR)]
        xa = [sb(f"xa{i}", [128, 512], BF16) for i in range(NR)]
        ttmp = sb("ttmp", [128, 512], BF16)
        Gb = sb("Gb", [128, D], BF16)
        Ga = sb("Ga", [128, 512], BF16)
        Tm = sb("Tm", [128, 2, 512], BF16)
        PT2 = [sb(f"PT{i}", [128, 5, 512], BF16) for i in range(2)]
        pooled = sb("pooled", [128, 512], BF16)
        ybT = sb("ybT", [128, 8, 128], BF16)
        mgT = sb("mgT", [128, 8, 128], BF16)
        mu = sb("mu", [128, 512], F32)
        mv = sb("mv", [128, 512], F32)
        small = sb("small", [128, 32], F32)


        for i in range(NR):
            E("pool", lambda e, i=i: e.memset(Vr[i][:, :, 64:66], 1.0), writes=[bf(f"Vr{i}")])
        for b in range(BPC):
            E("pool", lambda e, b=b: e.memset(vc[b][:, :, :, 64:66], 1.0), writes=[bf("vc")])

        c_eps1 = float(D * EPS)
        c_eps2 = float(D * 16.0 * EPS)

        def load_x(slot, src_ap, rope_idx=None):
            s = dsem(f"d_x{slot}")
            E("sp", lambda e: e.dma_start(out=xs_[slot][:], in_=src_ap), writes=[bf(f"xt{slot}")], dma_sem=s)
            if rope_idx is not None:
                s2 = dsem(f"d_rope{slot}")
                E("sp", lambda e: e.dma_start(out=ropet[slot][:], in_=rope_d[rope_idx]), writes=[bf(f"ropet{slot}")], dma_sem=s2)

        def norm_front(slot, hslot):
            xt = xs_[slot]
            xb = bf(f"xt{slot}")
            xsb = xsb2[hslot]
            xsbb = bf("xsb")
            E("act", lambda e: e.activation(out=xsb[:], in_=xt[:], func=AF.Square, accum_out=small[:, 0:1]),
              reads=[xb], writes=[xsbb, bf("ss")])
            E("dve", lambda e: e.tensor_scalar_add(out=small[:, 1:2], in0=small[:, 0:1], scalar1=c_eps1), reads=[bf("ss")], writes=[bf("ssb")])
            E("pool", lambda e: e.tensor_tensor(out=small[:, 2:3], in0=small[:, 1:2], in1=mh[:], op=ALU.pow),
              reads=[bf("ssb"), bf("mh")], writes=[bf("rstd")])
            E("act", lambda e: e.activation(out=xsb[:], in_=xt[:], func=AF.Copy, scale=small[:, 2:3]),
              reads=[xb, bf("rstd")], writes=[xsbb])

        def norm_back(hslot, h3, j):
            xsb = xsb2[hslot]
            xsbb = bf("xsb")
            tp, tpb = next_tbank()
            for k in range(8):
                E("pe", lambda e, k=k: e.transpose(out=tp[:, k * 128:(k + 1) * 128], in_=xsb[:, k * 128:(k + 1) * 128], identity=ident[:]),
                  reads=[xsbb, bf("ident")], writes=[tpb], inc=(k == 7))
            for k in range(8):
                hb = bf(f"hT{h3}_{k}")
                E("dve", lambda e, k=k: e.tensor_scalar(out=hT[h3][:, k, :], in0=tp[:, k * 128:(k + 1) * 128],
                                                        scalar1=gs[:, k * 4 + j:k * 4 + j + 1], scalar2=sh[:, k * 4 + j:k * 4 + j + 1],
                                                        op0=ALU.mult, op1=ALU.add),
                  reads=[tpb, bf("gs"), bf("sh")], writes=[hb])

        def norm_to_hT(slot, hslot, h3, j):
            norm_front(slot, hslot)
            norm_back(hslot, h3, j)

        def hT_bufs(h3):
            return [bf(f"hT{h3}_{k}") for k in range(8)]

        def proj_tok(h3, nb):
            pb, pbb = next_bank()
            for k in range(8):
                E("pe", lambda e, k=k: e.matmul(out=pb[:, :], lhsT=hT[h3][:, k, :], rhs=Wtok[:, k, nb * 512:(nb + 1) * 512],
                                                start=(k == 0), stop=(k == 7)),
                  reads=[bf(f"hT{h3}_{k}"), bf("Wtok")], writes=[pbb], inc=(k == 7))
            return pb, pbb

        def rope(pb, pbb, nheads, rslot, out_t, out_off, out_buf, add_eng="pool"):
            n = nheads * 64
            rb = bf(f"ropet{rslot}")
            cc = view(ropet[rslot], 0, [[0, nheads], [1, 64]])
            qv = view(pb, 0, [[64, nheads], [1, 64]])
            E("dve", lambda e: e.tensor_tensor(out=view(rt1, 0, [[64, nheads], [1, 64]]), in0=qv, in1=cc, op=ALU.mult),
              reads=[pbb, rb], writes=[bf("rt1")])
            for a in range(2):
                E("dve", lambda e, a=a: e.tensor_tensor(out=view(rt2, 32 * a, [[64, nheads], [16, 2], [1, 16]]),
                                                        in0=view(pb, 32 * a + 16, [[64, nheads], [-16, 2], [1, 16]]),
                                                        in1=view(ropet[rslot], 64 + 32 * a, [[0, nheads], [16, 2], [1, 16]]), op=ALU.mult),
                  reads=[pbb, rb], writes=[bf(f"rt2_{a}")])
            E(add_eng, lambda e: e.tensor_tensor(out=out_t[:, out_off:out_off + n], in0=rt1[:, 0:n], in1=rt2[:, 0:n], op=ALU.add),
              reads=[bf("rt1"), bf("rt2_0"), bf("rt2_1")], writes=[out_buf])

        def stage1(b, i, slot, hslot, h3, rslot, do_back=True):
            if do_back:
                norm_back(hslot, h3, b)
            ring = i % NR
            pb, pbb = proj_tok(h3, 2)
            rope(pb, pbb, 4, rslot, krot, 0, bf("krot"), add_eng="dve")
            E("act", lambda e: e.activation(out=Vr[ring][:, :, 0:64], in_=view(pb, 256, [[64, 4], [1, 64]]), func=AF.Copy),
              reads=[pbb], writes=[bf(f"Vr{ring}")])
            pb2, pbb2 = proj_tok(h3, 3)
            E("act", lambda e: e.activation(out=xa[ring][:], in_=pb2[:, :], func=AF.Copy), reads=[pbb2], writes=[bf(f"xa{ring}")])
            for nb in range(2):
                pbq, pbbq = proj_tok(h3, nb)
                rope(pbq, pbbq, 8, rslot, qrot, nb * 512, bf(f"qrot{nb}"))
            tp2, tpb2 = next_tbank()
            for c in range(2):
                E("pe", lambda e, c=c: e.transpose(out=tp2[:, c * 128:(c + 1) * 128], in_=krot[:, c * 128:(c + 1) * 128], identity=ident[:]),
                  reads=[bf("krot"), bf("ident")], writes=[tpb2], inc=(c == 1))
            E("dve", lambda e: e.tensor_copy(out=view(kT[ring], 0, [[1, 256]]), in_=tp2[:, 0:256]), reads=[tpb2], writes=[bf(f"kT{ring}")])

        def stage1_qT(hslot):
            tp, tpb = next_tbank()
            for c in range(8):
                E("pe", lambda e, c=c: e.transpose(out=tp[:, c * 128:(c + 1) * 128], in_=qrot[:, c * 128:(c + 1) * 128], identity=ident[:]),
                  reads=[bf(f"qrot{c // 4}"), bf("ident")], writes=[tpb], inc=(c == 7))
            E("dve", lambda e: e.tensor_copy(out=view(qT[hslot], 0, [[1, 1024]]), in_=tp[:, :]), reads=[tpb], writes=[bf(f"qT{hslot}")])

        def stage_ctx(b, t, slot, hslot, h3):
            pb, pbb = proj_tok(h3, 2)
            E("act", lambda e: e.activation(out=krot[:], in_=pb[:, 0:256], func=AF.Copy), reads=[pbb], writes=[bf("krot")])
            E("act", lambda e: e.activation(out=vc[b][:, t, :, 0:64], in_=view(pb, 256, [[64, 4], [1, 64]]), func=AF.Copy),
              reads=[pbb], writes=[bf("vc")])
            tp2, tpb2 = next_tbank()
            for c in range(2):
                E("pe", lambda e, c=c: e.transpose(out=tp2[:, c * 128:(c + 1) * 128], in_=krot[:, c * 128:(c + 1) * 128], identity=ident[:]),
                  reads=[bf("krot"), bf("ident")], writes=[tpb2], inc=(c == 1))
            E("dve", lambda e: e.tensor_copy(out=kTc[b][:, :, t * 128:(t + 1) * 128], in_=view(tp2, 0, [[128, 2], [1, 128]])),
              reads=[tpb2], writes=[bf("kTc")])

        def stage2(b, i, slot, hslot, h3, mid_hook=None, late_hook=None, early_hook=None):
            ring = i % NR
            hbufs = hT_bufs(h3)

            def feat_group(ch0, evac):
                pb, pbb = next_bank()
                for c4 in range(4):
                    ch = ch0 + c4
                    for k in range(8):
                        E("pe", lambda e, pb=pb, c4=c4, ch=ch, k=k: e.matmul(
                            out=pb[:, c4 * 128:(c4 + 1) * 128], lhsT=Wfeat[:, k, ch * 128:(ch + 1) * 128], rhs=hT[h3][:, k, :],
                            start=(k == 0), stop=(k == 7)),
                          reads=[hbufs[k], bf("Wfeat")], writes=[pbb], inc=(k == 7 and c4 == 3))
                evac(pb, pbb)

            def do_gb(nbs=(0, 1)):
                for nb in nbs:
                    pb, pbb = proj_tok(h3, 4 + nb)
                    E("act", lambda e, pb=pb: e.activation(out=ttmp[:], in_=pb[:, :], func=AF.Tanh, scale=0.5), reads=[pbb], writes=[bf("ttmp")])
                    E("dve", lambda e, pb=pb, nb=nb: e.scalar_tensor_tensor(out=Gb[:, nb * 512:(nb + 1) * 512], in0=ttmp[:], scalar=1.0, in1=pb[:, :],
                                                                           op0=ALU.add, op1=ALU.mult),
                      reads=[bf("ttmp"), pbb], writes=[bf(f"Gb{nb}")])

            def do_ga():
                def ev(pb, pbb):
                    E("act", lambda e: e.activation(out=ttmp[:], in_=pb[:, :], func=AF.Tanh, scale=0.5), reads=[pbb], writes=[bf("ttmp")])
                    E("dve", lambda e: e.scalar_tensor_tensor(out=Ga[:], in0=ttmp[:], scalar=1.0, in1=pb[:, :], op0=ALU.add, op1=ALU.mult),
                      reads=[bf("ttmp"), pbb], writes=[bf("Ga")])
                feat_group(0, ev)

            def do_pool1():
                pb, pbb = next_bank()
                for g in range(4):
                    terms = []
                    if i > 0:
                        terms.append(((i - 1) % NR, 5 * g + 3))
                    terms.append((ring, 5 * g + (1 if i == 0 else (2 if i == NT - 1 else 0))))
                    if i < NT - 1:
                        terms.append(((i + 1) % NR, 5 * g + 4))
                    for ti, (rs, mi) in enumerate(terms):
                        E("pe", lambda e, g=g, rs=rs, mi=mi, ti=ti, nterm=len(terms): e.matmul(
                            out=pb[:, g * 128:(g + 1) * 128], lhsT=xa[rs][:, g * 128:(g + 1) * 128], rhs=band[:, mi, :],
                            start=(ti == 0), stop=(ti == nterm - 1)),
                          reads=[bf(f"xa{rs}"), bf("band")], writes=[pbb], inc=(g == 3 and ti == len(terms) - 1))
                E("dve", lambda e: e.tensor_copy(out=pooled[:], in_=pb[:, :]), reads=[pbb], writes=[bf("pooled")])

            def do_pool2():
                pb2, pbb2 = next_bank()
                for g in range(4):
                    E("pe", lambda e, g=g: e.matmul(out=pb2[:, g * 128:(g + 1) * 128], lhsT=Wp[:, g, :], rhs=pooled[:, g * 128:(g + 1) * 128],
                                                    start=True, stop=True),
                      reads=[bf("pooled"), bf("Wp")], writes=[pbb2], inc=(g == 3))
                E("dve", lambda e: e.tensor_tensor(out=pooled[:], in0=pb2[:, :], in1=Ga[:], op=ALU.mult),
                  reads=[pbb2, bf("Ga")], writes=[bf("pooled")])

            blocks = []
            if i > 0:
                blocks.append(("lat", (i - 1) % NR, 0))
            blocks.append(("lat", ring, None))
            if i < NT - 1:
                blocks.append(("lat", (i + 1) % NR, 1))
            blocks.append(("ctx", 0, None))
            blocks.append(("ctx", 1, None))
            nblk = len(blocks)

            def do_qk(gs_, jbs):
                for jb in jbs:
                    kind, idx, mk = blocks[jb]
                    banks = []
                    for g in gs_:
                        p0 = 64 * (g % 2)
                        cq = 4 * (g // 2)
                        ck = g // 2
                        rhs_q = pview(qT[hslot], p0, 64, cq * 128, [[128, 4], [1, 128]])
                        sbk, sbb = next_bank()
                        banks.append((g, sbk, sbb))
                        if kind == "lat":
                            lk = pview(kT[idx], p0, 64, ck * 128, [[1, 128]])
                            kb = bf(f"kT{idx}")
                        else:
                            lk = pview(kTc[b], p0, 64, ck * CTX + idx * 128, [[1, 128]])
                            kb = bf("kTc")
                        E("pe", lambda e, sbk=sbk, lk=lk, mk=mk, rhs_q=rhs_q: e.matmul(out=sbk[:, :], lhsT=lk, rhs=rhs_q, start=True, stop=(mk is None)),
                          reads=[kb, bf(f"qT{hslot}")], writes=[sbb], inc=(mk is None))
                    for g, sbk, sbb in banks:
                        if mk is not None:
                            for r in range(4):
                                E("pe", lambda e, sbk=sbk, mk=mk, r=r: e.matmul(out=sbk[:, r * 128:(r + 1) * 128], lhsT=ident[:], rhs=maskb[:, mk, :],
                                                                               start=False, stop=(r == 3)),
                                  reads=[bf("ident"), bf("maskb")], writes=[sbb], inc=(r == 3))
                        E("act", lambda e, sbk=sbk, jb=jb, g=g: e.activation(out=PT2[g % 2][:, jb, :], in_=sbk[:, :], func=AF.Exp, scale=0.125),
                          reads=[sbb], writes=[bf(f"PT{g % 2}_{jb}")])

            def do_pv(g):
                ob, obb = next_bank()
                for r in range(4):
                    for jb, (kind, idx, mk) in enumerate(blocks):
                        if kind == "lat":
                            rv = pview(Vr[idx], 0, 128, g * 66, [[1, 65]])
                            vb_ = bf(f"Vr{idx}")
                        else:
                            rv = pview(vc[b], 0, 128, (idx * 4 + g) * 66, [[1, 65]])
                            vb_ = bf("vc")
                        E("pe", lambda e, r=r, jb=jb, rv=rv: e.matmul(out=ob[:, r * 128:r * 128 + 65], lhsT=PT2[g % 2][:, jb, r * 128:(r + 1) * 128], rhs=rv,
                                                                     start=(jb == 0), stop=(jb == nblk - 1)),
                          reads=[bf(f"PT{g % 2}_{jb}"), vb_], writes=[obb], inc=(r == 3 and jb == nblk - 1))
                hoff = 8 * (g // 2) + (g % 2)
                E("dve", lambda e: e.tensor_tensor(out=small[:, 8:12], in0=view(ob, 64, [[128, 4]]),
                                                   in1=view(esink, hoff, [[2, 4]]), op=ALU.add),
                  reads=[obb, bf("esink")], writes=[bf("den")])
                E("dve", lambda e: e.reciprocal(out=small[:, 12:16], in_=small[:, 8:12]), reads=[bf("den")], writes=[bf("rec")])
                for r in range(4):
                    co = (hoff + 2 * r) * 64
                    extra = [bf("qrot0"), bf("qrot1")] if (g == 0 and r == 0) else []
                    E("dve", lambda e, r=r, co=co: e.scalar_tensor_tensor(out=yb[:, co:co + 64], in0=ob[:, r * 128:r * 128 + 64],
                                                                         scalar=small[:, 12 + r:13 + r], in1=Gb[:, co:co + 64],
                                                                         op0=ALU.mult, op1=ALU.mult),
                      reads=[obb, bf("rec"), bf("Gb0"), bf("Gb1")], writes=[bf(f"yb{g}_{r}")] + extra)

            def do_ybT():
                tp, tpb = next_tbank()
                for c in range(8):
                    E("pe", lambda e, c=c: e.transpose(out=tp[:, c * 128:(c + 1) * 128], in_=yb[:, c * 128:(c + 1) * 128], identity=ident[:]),
                      reads=[bf(f"yb{g}_{r}") for g in range(4) for r in range(4)] + [bf("qrot0"), bf("qrot1"), bf("ident")], writes=[tpb], inc=(c == 7))
                for hh in range(2):
                    E("dve", lambda e, hh=hh: e.tensor_copy(out=view(ybT, hh * 512, [[1, 512]]), in_=tp[:, hh * 512:(hh + 1) * 512]),
                      reads=[tpb], writes=[bf(f"ybT{hh}")])

            def do_A(rd):
                pa, pab = next_bank()
                for c4 in range(4):
                    cd = rd * 4 + c4
                    for k in range(4):
                        E("pe", lambda e, c4=c4, cd=cd, k=k: e.matmul(out=pa[:, c4 * 128:(c4 + 1) * 128], lhsT=Wa[:, k, cd * 128:(cd + 1) * 128],
                                                                     rhs=pooled[:, k * 128:(k + 1) * 128], start=(k == 0), stop=(k == 3)),
                          reads=[bf("pooled"), bf("Wa")], writes=[pab], inc=(k == 3 and c4 == 3))
                return pa, pab

            def do_B(rd):
                pbk, pbkb = next_bank()
                for c4 in range(4):
                    cd = rd * 4 + c4
                    for k in range(8):
                        E("pe", lambda e, c4=c4, cd=cd, k=k: e.matmul(out=pbk[:, c4 * 128:(c4 + 1) * 128], lhsT=Wb[:, k, cd * 128:(cd + 1) * 128],
                                                                     rhs=ybT[:, k, :], start=(k == 0), stop=(k == 7)),
                          reads=[bf(f"ybT{k // 4}"), bf("Wb")], writes=[pbkb], inc=(k == 7 and c4 == 3))
                return pbk, pbkb

            def do_gm(rd, halves=(0, 1)):
                for half in halves:
                    def ev(pgm, pgmb, half=half):
                        E("act", lambda e: e.activation(out=Tm[:, half, :], in_=pgm[:, :], func=AF.Tanh, scale=0.5),
                          reads=[pgmb], writes=[bf(f"Tm{half}")])
                    feat_group(4 + half * 8 + rd * 4, ev)

            def do_u(pa, pab):
                E("dve", lambda e: e.scalar_tensor_tensor(out=mu[:], in0=Tm[:, 0, :], scalar=1.0, in1=pa[:, :], op0=ALU.add, op1=ALU.mult),
                  reads=[bf("Tm0"), pab], writes=[bf("mu")])

            def do_v_merge(rd, pbk, pbkb):
                E("dve", lambda e: e.scalar_tensor_tensor(out=mv[:], in0=Tm[:, 1, :], scalar=1.0, in1=pbk[:, :], op0=ALU.add, op1=ALU.mult),
                  reads=[bf("Tm1"), pbkb], writes=[bf("mv")])
                E("pool", lambda e: e.tensor_tensor(out=view(mgT, rd * 512, [[1, 512]]), in0=mu[:], in1=mv[:], op=ALU.add),
                  reads=[bf("mu"), bf("mv")], writes=[bf(f"mgT{rd}")])

            def do_out():
                obs = [next_bank() for nb in range(2)]
                for kh in range(2):
                    for nb in range(2):
                        pb, pbb = obs[nb]
                        for k in range(4 * kh, 4 * kh + 4):
                            E("pe", lambda e, pb=pb, nb=nb, k=k: e.matmul(out=pb[:, :], lhsT=mgT[:, k, :], rhs=Wo[:, k, nb * 512:(nb + 1) * 512],
                                                                         start=(k == 0), stop=(k == 7)),
                              reads=[bf(f"mgT{k // 4}"), bf("Wo")], writes=[pbb], inc=(k == 7))
                for nb in range(2):
                    pb, pbb = obs[nb]
                    E("act", lambda e, pb=pb, nb=nb: e.activation(out=ttmp[:], in_=pb[:, :], func=AF.Square, accum_out=small[:, 3 + nb:4 + nb]),
                      reads=[pbb], writes=[bf("ttmp"), bf(f"ss2_{nb}")])
                E("dve", lambda e: e.tensor_tensor(out=small[:, 7:8], in0=small[:, 3:4], in1=small[:, 4:5], op=ALU.add),
                  reads=[bf("ss2_0"), bf("ss2_1")], writes=[bf("ss2t")])
                E("dve", lambda e: e.tensor_scalar_add(out=small[:, 5:6], in0=small[:, 7:8], scalar1=c_eps2),
                  reads=[bf("ss2t")], writes=[bf("ss2s")])
                E("pool", lambda e: e.tensor_tensor(out=small[:, 6:7], in0=small[:, 5:6], in1=mh[:], op=ALU.pow),
                  reads=[bf("ss2s"), bf("mh")], writes=[bf("r2")])
                for nb in range(2):
                    pb, pbb = obs[nb]
                    E("dve", lambda e, pb=pb, nb=nb: e.scalar_tensor_tensor(out=mu[:], in0=pb[:, :], scalar=small[:, 6:7], in1=gg[:, nb * 512:(nb + 1) * 512],
                                                                           op0=ALU.mult, op1=ALU.mult),
                      reads=[pbb, bf("r2"), bf("gg")], writes=[bf("mu")])
                    E("pool", lambda e, nb=nb: e.tensor_tensor(out=xs_[slot][:, nb * 512:(nb + 1) * 512], in0=xs_[slot][:, nb * 512:(nb + 1) * 512],
                                                               in1=mu[:], op=ALU.add),
                      reads=[bf("mu"), bf(f"xt{slot}")], writes=[bf(f"xt{slot}")])
                s_ = dsem(f"d_y{slot}")
                E("sp", lambda e: e.dma_start(out=y_d[b, i * 128:(i + 1) * 128, :], in_=xs_[slot][:]), reads=[bf(f"xt{slot}")], writes=[bf(f"xt{slot}")],
                  dma_sem=s_)

            jall = list(range(nblk))
            c1, c2, c3 = jall[0:2], jall[2:4], jall[4:]
            do_gb((0,))
            do_qk((0, 1), c1)
            do_gb((1,))
            do_qk((0, 1), c2)
            do_qk((0, 1), c3)
            do_ga()
            if early_hook is not None:
                early_hook()
            do_pool1()
            do_pv(0)
            do_pv(1)
            do_qk((2, 3), c1)
            do_pool2()
            do_gm(0, (0,))
            do_qk((2, 3), c2)
            do_qk((2, 3), c3)
            if mid_hook is not None:
                mid_hook()
            pa0 = do_A(0)
            do_gm(0, (1,))
            do_u(*pa0)
            do_pv(2)
            do_pv(3)
            do_gm(1, (0,))
            do_ybT()
            pa1 = do_A(1)
            pb0 = do_B(0)
            do_v_merge(0, *pb0)
            do_gm(1, (1,))
            pb1 = do_B(1)
            do_u(*pa1)
            do_v_merge(1, *pb1)
            if late_hook is not None:
                late_hook()
            do_out()

        work = []
        for b in range(BPC):
            for t in range(2):
                work.append(("ctx", b, t))
            for i in range(NT):
                work.append(("lat", b, i))
        gidx = {w: n for n, w in enumerate(work)}

        def issue_load(w):
            kind, b, i = w
            slot = gidx[w] % NX
            if kind == "ctx":
                load_x(slot, ctx_d[b, i * 128:(i + 1) * 128, :])
            else:
                load_x(slot, x_d[b, i * 128:(i + 1) * 128, :], rope_idx=i)

        issue_load(work[0])
        front_done = set()
        back_done = set()

        def front(w):
            if w is None or w in front_done:
                return
            front_done.add(w)
            norm_front(gidx[w] % NX, gidx[w] % 2)

        def back(w):
            if w is None or w in back_done:
                return
            front(w)
            back_done.add(w)
            norm_back(gidx[w] % 2, gidx[w] % 3, (2 if w[0] == "ctx" else w[1]))

        for n, w in enumerate(work):
            kind, b, i = w
            if n + 1 < len(work):
                issue_load(work[n + 1])
            slot = gidx[w] % NX
            hslot = gidx[w] % 2
            h3 = gidx[w] % 3
            if kind == "ctx":
                if i == 0:
                    E("sp", lambda e, b=b: e.dma_start(out=gg[:], in_=ggs_d[b:b + 1, :].partition_broadcast(128)),
                      reads=[bf("ggs")], writes=[bf("gg")], dma_sem=dsem("d_gg"))
                back(w)
                front(work[n + 1])
                stage_ctx(b, i, slot, hslot, h3)
                continue
            back(w)
            nxt = work[n + 1] if n + 1 < len(work) else None
            if i == 0:
                front(nxt)
            stage1(b, i, slot, hslot, h3, slot, do_back=False)
            if nxt is not None and nxt[0] != "lat":
                nxt = None
            qt_hook = (lambda hslot=hslot: stage1_qT(hslot))
            if i > 0:
                wp = ("lat", b, i - 1)
                stage2(b, i - 1, gidx[wp] % NX, gidx[wp] % 2, gidx[wp] % 3,
                       mid_hook=(lambda nxt=nxt: front(nxt)), late_hook=(lambda nxt=nxt: back(nxt)), early_hook=qt_hook)
            else:
                stage1_qT(hslot)
            if i == NT - 1:
                stage2(b, i, slot, hslot, h3)

        tr.final_wait("sp", [k for k in dma_sems if k.startswith("d_y")])
        print("sbuf remaining", nc.sbuf_bytes_remaining, {e: len(tr.ops[e]) for e in ENGINES})

        sems = {}
        for k in list(ENGINES) + dma_sems:
            sems[k] = es.enter_context(nc.semaphore(k))
        with nc.Block() as block:
            @block.sync
            def _(e):
                tr.replay_engine("sp", e, sems)

            @block.scalar
            def _(e):
                tr.replay_engine("act", e, sems)

            @block.vector
            def _(e):
                tr.replay_engine("dve", e, sems)

            @block.gpsimd
            def _(e):
                tr.replay_engine("pool", e, sems)

            @block.tensor
            def _(e):
                tr.replay_engine("pe", e, sems)
    return nc


def _prep_shared(w_mod, b_mod, norm_pre_g, norm_post_g, w_in, pool_w, pool_scale, sink, w_branch_a, w_branch_b, w_out, L):
    f = lambda a: np.ascontiguousarray(np.asarray(a, dtype=np.float32))
    w_in = f(w_in)[0]
    hp = np.array(HP)
    colperm = (hp[:, None] * 64 + np.arange(64)[None, :]).reshape(-1)
    xa_c = w_in[:, 0:512]
    ga_c = w_in[:, 512:1024]
    q_c = w_in[:, 1024:2048][:, colperm]
    k_c = w_in[:, 2048:2304]
    v_c = w_in[:, 2304:2560]
    gb_c = w_in[:, 2560:3584][:, colperm]
    gm_c = w_in[:, 3584:5632]
    w_tok = f(np.concatenate([q_c, k_c, v_c, xa_c, gb_c], axis=1))
    w_feat = f(np.concatenate([ga_c, gm_c], axis=1))
    b_mod = f(b_mod)[0]
    shared = {
        "w_tok": w_tok,
        "w_feat": w_feat,
        "w_a": f(w_branch_a)[0],
        "w_b": f(f(w_branch_b)[0][colperm, :]),
        "w_o": f(w_out)[0],
        "pool_w": f(f(pool_w)[0].transpose(1, 0, 2).reshape(128, 512)),
        "w_mod": f(w_mod)[0],
        "b_modT": f(b_mod[0:2048].reshape(16, 128).T),
        "b_gate": f(b_mod[2048:3072].reshape(1, D)),
        "g_preT": f(f(norm_pre_g)[0].reshape(8, 128).T),
        "g_post": f(f(norm_post_g)[0].reshape(1, D)),
        "pool_scale": f(f(pool_scale)[0].reshape(1, 512)),
        "sink": f(f(sink)[0][hp].reshape(1, 16)),
        "rope": rope_tables(L),
        "band": band_tables(),
        "mask": mask_tables(),
    }
    return shared


_PROGRAMS = {}


def kernel_impl(x, c, ctx, c_ctx, w_mod, b_mod, norm_pre_g, norm_post_g, w_in, pool_w, pool_scale,
                sink, w_branch_a, w_branch_b, w_out):
    x = np.asarray(x, dtype=np.float32)
    c = np.asarray(c, dtype=np.float32)
    ctx = np.asarray(ctx, dtype=np.float32)
    c_ctx = np.asarray(c_ctx, dtype=np.float32)
    Bn, L, _ = x.shape
    assert Bn == NCORES * BPC
    shared = _prep_shared(w_mod, b_mod, norm_pre_g, norm_post_g, w_in, pool_w, pool_scale, sink,
                          w_branch_a, w_branch_b, w_out, L)
    in_maps = []
    for core in range(NCORES):
        bs = slice(core * BPC, (core + 1) * BPC)
        c3 = np.zeros((4, D), np.float32)
        c3[0:BPC] = c[bs]
        c3[2] = c_ctx
        cT = np.ascontiguousarray(c3.reshape(4, 8, 128).transpose(2, 1, 0).reshape(128, 32))
        m = dict(shared)
        m["x"] = np.ascontiguousarray(x[bs])
        m["ctx"] = np.ascontiguousarray(ctx[bs])
        m["cT"] = cT
        in_maps.append(m)
    if L not in _PROGRAMS:
        _PROGRAMS[L] = build_program(L)
    nc = _PROGRAMS[L]
    res = run_bass_kernel_spmd(nc, in_maps, core_ids=list(range(NCORES)))
    out = np.concatenate([np.asarray(r["y"], dtype=np.float32) for r in res.results], axis=0)
    return out


def kernel(**inputs):
    return kernel_impl(**inputs)
```
